# Optimizing a Trainium2 kernel written in Bass

```python
import jax, jax.numpy as jnp
from jax import lax
import numpy as np

D_MODEL = 1024
BATCH = 4
SEQ = 4096
DEPTH = 1

N_META = 16
GRID_W = 64
NA_HEADS = 8
NA_HEAD_DIM = 64
NA_WIN_H = 8
NA_WIN_W = 16
NA_QBLK_W = 16
NA_KBLK_W = 32
NA_W = NA_HEADS * NA_HEAD_DIM
MLA_HEADS = 8
MLA_NOPE_DIM = 64
MLA_ROPE_DIM = 32
MLA_V_DIM = 64
MLA_Q_RANK = 384
MLA_KV_RANK = 256
MLA_QBLK = 128
MLA_W = MLA_HEADS * MLA_V_DIM
ROPE_THETA = 10000.0
D_FF = 4 * D_MODEL
EPS = 1e-6

IN_SIZES = (NA_W, NA_W, NA_W, MLA_Q_RANK, MLA_KV_RANK, MLA_ROPE_DIM, D_MODEL, D_MODEL)
D_IN = sum(IN_SIZES)
IN_SPLITS = tuple(int(s) for s in np.cumsum(IN_SIZES)[:-1])

kernel_name = "hybrid_na_mla_gated_encoder"


def rmsnorm(x, g):
    xf = x.astype(jnp.float32)
    y = xf * lax.rsqrt(jnp.mean(xf * xf, axis=-1, keepdims=True) + EPS)
    return (y * g.astype(jnp.float32)).astype(x.dtype)


def rope(x, cos, sin):
    half = x.shape[-1] // 2
    x1, x2 = x[..., :half], x[..., half:]
    return jnp.concatenate([x1 * cos - x2 * sin, x2 * cos + x1 * sin], axis=-1).astype(x.dtype)


def neighborhood_attention(q, k, v, rpb):
    B, L, H, dh = q.shape
    n_tok = L - N_META
    rows = n_tok // GRID_W
    kh = min(NA_WIN_H, rows)
    scale = dh ** -0.5
    qm, km, vm = q[:, :N_META], k[:, :N_META], v[:, :N_META]
    qg = q[:, N_META:].reshape(B, rows, GRID_W, H, dh)
    kg = k[:, N_META:].reshape(B, rows, GRID_W, H, dh)
    vg = v[:, N_META:].reshape(B, rows, GRID_W, H, dh)

    s_m = jnp.einsum('bqhd,bkhd->bhqk', qm, km, preferred_element_type=jnp.float32) * scale
    p_m = jax.nn.softmax(s_m, axis=-1).astype(vm.dtype)
    out_meta = jnp.einsum('bhqk,bkhd->bqhd', p_m, vm)

    n_cb = GRID_W // NA_QBLK_W
    qcol = np.arange(GRID_W).reshape(n_cb, NA_QBLK_W)
    cstart = np.clip(qcol - NA_WIN_W // 2, 0, GRID_W - NA_WIN_W)
    kb0 = np.clip(np.arange(n_cb) * NA_QBLK_W - NA_WIN_W // 2, 0, GRID_W - NA_KBLK_W)
    kcol = kb0[:, None] + np.arange(NA_KBLK_W)
    kc = kcol[:, None, :]
    cvalid = (kc >= cstart[..., None]) & (kc < cstart[..., None] + NA_WIN_W)
    dc_idx = np.clip(kc - qcol[:, :, None] + NA_WIN_W - 1, 0, 2 * NA_WIN_W - 2)
    bias_c = rpb.astype(jnp.float32)[:, :, dc_idx]
    cvalid_b = jnp.asarray(cvalid)[None, None, :, :, None, :]

    def row_block(r):
        rs = jnp.clip(r - kh // 2, 0, rows - kh)
        k_rows = lax.dynamic_slice_in_dim(kg, rs, kh, axis=1)
        v_rows = lax.dynamic_slice_in_dim(vg, rs, kh, axis=1)
        k_blk = k_rows[:, :, kcol]
        v_blk = v_rows[:, :, kcol]
        q_row = lax.dynamic_index_in_dim(qg, r, axis=1, keepdims=False)
        q_row = q_row.reshape(B, n_cb, NA_QBLK_W, H, dh)
        s_grid = jnp.einsum('bnqhd,banchd->bhnqac', q_row, k_blk,
                            preferred_element_type=jnp.float32) * scale
        dr_idx = rs + jnp.arange(kh) - r + NA_WIN_H - 1
        bias = bias_c[:, dr_idx].transpose(0, 2, 3, 1, 4)
        s_grid = jnp.where(cvalid_b, s_grid + bias[None], -jnp.inf)
        s_grid = s_grid.reshape(B, H, n_cb, NA_QBLK_W, kh * NA_KBLK_W)
        s_meta = jnp.einsum('bnqhd,bmhd->bhnqm', q_row, km,
                            preferred_element_type=jnp.float32) * scale
        p = jax.nn.softmax(jnp.concatenate([s_meta, s_grid], axis=-1), axis=-1).astype(v.dtype)
        p_meta = p[..., :N_META]
        p_grid = p[..., N_META:].reshape(B, H, n_cb, NA_QBLK_W, kh, NA_KBLK_W)
        out = (jnp.einsum('bhnqm,bmhd->bnqhd', p_meta, vm)
               + jnp.einsum('bhnqac,banchd->bnqhd', p_grid, v_blk))
        return out.reshape(B, GRID_W, H, dh)

    out_grid = lax.map(row_block, jnp.arange(rows))
    out_grid = out_grid.transpose(1, 0, 2, 3, 4).reshape(B, n_tok, H, dh)
    return jnp.concatenate([out_meta, out_grid], axis=1)


def mla_attention(c_q, c_kv, k_rope_raw, q_norm, w_uq, kv_norm, w_ukv, cos, sin):
    B, L, _ = c_q.shape
    H = MLA_HEADS
    q = (rmsnorm(c_q, q_norm) @ w_uq).reshape(B, L, H, MLA_NOPE_DIM + MLA_ROPE_DIM)
    q_nope = q[..., :MLA_NOPE_DIM]
    q_rope = rope(q[..., MLA_NOPE_DIM:], cos[:, None, :], sin[:, None, :])
    kv = (rmsnorm(c_kv, kv_norm) @ w_ukv).reshape(B, L, H, MLA_NOPE_DIM + MLA_V_DIM)
    k_nope, v = kv[..., :MLA_NOPE_DIM], kv[..., MLA_NOPE_DIM:]
    k_rope = rope(k_rope_raw, cos, sin)
    scale = (MLA_NOPE_DIM + MLA_ROPE_DIM) ** -0.5

    def attend(qn, qr):
        s = (jnp.einsum('bqhd,bkhd->bhqk', qn, k_nope, preferred_element_type=jnp.float32)
             + jnp.einsum('bqhr,bkr->bhqk', qr, k_rope, preferred_element_type=jnp.float32)) * scale
        p = jax.nn.softmax(s, axis=-1).astype(v.dtype)
        return jnp.einsum('bhqk,bkhd->bqhd', p, v)

    out_meta = attend(q_nope[:, :N_META], q_rope[:, :N_META])
    n_tok = L - N_META
    n_blk = n_tok // MLA_QBLK
    qn_b = q_nope[:, N_META:].reshape(B, n_blk, MLA_QBLK, H, MLA_NOPE_DIM).swapaxes(0, 1)
    qr_b = q_rope[:, N_META:].reshape(B, n_blk, MLA_QBLK, H, MLA_ROPE_DIM).swapaxes(0, 1)
    out_blk = lax.map(lambda t: attend(t[0], t[1]), (qn_b, qr_b))
    out_blk = out_blk.swapaxes(0, 1).reshape(B, n_tok, H, MLA_V_DIM)
    return jnp.concatenate([out_meta, out_blk], axis=1).reshape(B, L, MLA_W)


def setup_inputs(seed: int = 0) -> dict:
    key = jax.random.key(seed)
    ks = jax.random.split(key, 18)
    f32 = jnp.float32

    def w(k, shape, fan_in):
        return jax.random.normal(k, shape, f32) * (fan_in ** -0.5)

    def gain(k, shape):
        return 1.0 + 0.05 * jax.random.normal(k, shape, f32)

    return {
        "x": jax.random.normal(ks[0], (BATCH, SEQ, D_MODEL), f32),
        "meta": jax.random.normal(ks[1], (N_META, D_MODEL), f32),
        "norm_mix": gain(ks[2], (DEPTH, D_MODEL)),
        "w_in": w(ks[3], (DEPTH, D_MODEL, D_IN), D_MODEL),
        "na_rpb": 0.1 * jax.random.normal(ks[4], (DEPTH, NA_HEADS, 2 * NA_WIN_H - 1, 2 * NA_WIN_W - 1), f32),
        "mla_q_norm": gain(ks[5], (DEPTH, MLA_Q_RANK)),
        "w_uq": w(ks[6], (DEPTH, MLA_Q_RANK, MLA_HEADS * (MLA_NOPE_DIM + MLA_ROPE_DIM)), MLA_Q_RANK),
        "mla_kv_norm": gain(ks[7], (DEPTH, MLA_KV_RANK)),
        "w_ukv": w(ks[8], (DEPTH, MLA_KV_RANK, MLA_HEADS * (MLA_NOPE_DIM + MLA_V_DIM)), MLA_KV_RANK),
        "w_na_out": w(ks[9], (DEPTH, NA_W, D_MODEL), NA_W),
        "w_mla_out": w(ks[10], (DEPTH, MLA_W, D_MODEL), MLA_W),
        "w_out": w(ks[11], (DEPTH, D_MODEL, D_MODEL), D_MODEL),
        "norm_ffn": gain(ks[12], (DEPTH, D_MODEL)),
        "w_ff1": w(ks[13], (DEPTH, D_MODEL, D_FF), D_MODEL),
        "w_ff2": w(ks[14], (DEPTH, D_FF, D_MODEL), D_FF),
        "norm_final": gain(ks[15], (D_MODEL,)),
    }


def reference(x, meta, norm_mix, w_in, na_rpb, mla_q_norm, w_uq, mla_kv_norm, w_ukv,
              w_na_out, w_mla_out, w_out, norm_ffn, w_ff1, w_ff2, norm_final):
    B, S, D = x.shape
    h = jnp.concatenate([jnp.broadcast_to(meta.astype(x.dtype)[None], (B, N_META, D)), x], axis=1)
    L = S + N_META
    pos = jnp.arange(L, dtype=jnp.float32)
    inv_freq = 1.0 / (ROPE_THETA ** (jnp.arange(0, MLA_ROPE_DIM, 2, dtype=jnp.float32) / MLA_ROPE_DIM))
    ang = pos[:, None] * inv_freq[None, :]
    cos, sin = jnp.cos(ang).astype(x.dtype), jnp.sin(ang).astype(x.dtype)

    for l in range(DEPTH):
        hn = rmsnorm(h, norm_mix[l])
        proj = hn @ w_in[l]
        q_na, k_na, v_na, c_q, c_kv, k_rope_raw, g_na, g_mla = jnp.split(proj, IN_SPLITS, axis=-1)
        o_na = neighborhood_attention(
            q_na.reshape(B, L, NA_HEADS, NA_HEAD_DIM),
            k_na.reshape(B, L, NA_HEADS, NA_HEAD_DIM),
            v_na.reshape(B, L, NA_HEADS, NA_HEAD_DIM),
            na_rpb[l]).reshape(B, L, NA_W) @ w_na_out[l]
        o_mla = mla_attention(c_q, c_kv, k_rope_raw, mla_q_norm[l], w_uq[l],
                              mla_kv_norm[l], w_ukv[l], cos, sin) @ w_mla_out[l]
        merged = jax.nn.sigmoid(g_na) * o_na + jax.nn.sigmoid(g_mla) * o_mla
        h = h + merged @ w_out[l]
        fn = rmsnorm(h, norm_ffn[l])
        h = h + jnp.square(jax.nn.relu(fn @ w_ff1[l])) @ w_ff2[l]

    return rmsnorm(h, norm_final)[:, N_META:]
```

```python
import os
import numpy as np
from contextlib import ExitStack
P3CUT = int(os.environ.get('P3CUT', '9'))
import concourse.bass as bass
import concourse.mybir as mybir
from concourse.bass_utils import run_bass_kernel_spmd

F32 = mybir.dt.float32
BF16 = mybir.dt.bfloat16
AF = mybir.ActivationFunctionType
ALU = mybir.AluOpType
EPS = 1e-6
NEG = -30000.0


class Buf:
    __slots__ = ("w", "r", "excl")

    def __init__(self, excl=False):
        self.w = None
        self.r = []
        self.excl = excl


class Op:
    __slots__ = ("eng", "fn", "deps", "sig", "ticket", "is_dma", "semkey", "semval", "idx")


class Prog:
    ENGS = ("pe", "act", "dve", "pool", "sp")

    def __init__(self, nc):
        self.nc = nc
        self.ops = []
        self.last = {}
        self.pending_dma = []

    enabled = True

    def op(self, eng, fn, reads=(), writes=(), dma=False, semkey=None):
        if not self.enabled:
            return None
        o = Op()
        o.eng, o.fn, o.is_dma, o.sig, o.ticket, o.semval = eng, fn, dma, False, None, None
        o.idx = len(self.ops)
        deps = {}
        for b in reads:
            if b.w is not None:
                deps[b.w.idx] = b.w
            if b.excl:
                lastr = {}
                for r in b.r:
                    if r.eng != eng and (r.eng not in lastr or lastr[r.eng].idx < r.idx):
                        lastr[r.eng] = r
                for r in lastr.values():
                    deps[r.idx] = r
        for b in writes:
            if b.w is not None:
                deps[b.w.idx] = b.w
            lastr = {}
            for r in b.r:
                if r.is_dma:
                    deps[r.idx] = r
                elif r.eng not in lastr or lastr[r.eng].idx < r.idx:
                    lastr[r.eng] = r
            for r in lastr.values():
                deps[r.idx] = r
        o.deps = list(deps.values())
        for b in reads:
            b.r.append(o)
        for b in writes:
            b.w = o
            b.r = []
        o.semkey = None
        if dma:
            o.semkey = semkey if semkey is not None else (writes[0] if writes else reads[0])
            self.pending_dma.append(o)
        self.last[eng] = o
        self.ops.append(o)
        return o

    def barrier(self):
        deps = [o for o in self.last.values()] + list(self.pending_dma)
        for e in self.ENGS:
            o = Op()
            o.eng, o.fn, o.is_dma, o.sig, o.ticket, o.semval, o.semkey = e, None, False, False, None, None, None
            o.idx = len(self.ops)
            o.deps = list(deps)
            self.ops.append(o)
        self.pending_dma = []
        self.last = {}

    def emit(self, es):
        nc = self.nc
        for o in self.ops:
            for d in o.deps:
                if d.is_dma:
                    continue
                if d.eng == o.eng == "pe" and not o.is_dma:
                    continue
                d.sig = True
        engsem = {e: es.enter_context(nc.semaphore("s_" + e)) for e in self.ENGS}
        cnt = {e: 0 for e in self.ENGS}
        dmasem, dmacnt = {}, {}
        for o in self.ops:
            if o.is_dma:
                k = id(o.semkey)
                if k not in dmasem:
                    dmasem[k] = es.enter_context(nc.semaphore("d%d" % len(dmasem)))
                    dmacnt[k] = 0
                dmacnt[k] += 16
                o.semval = dmacnt[k]
            elif o.sig:
                cnt[o.eng] += 1
                o.ticket = cnt[o.eng]
        per = {e: [o for o in self.ops if o.eng == e] for e in self.ENGS}
        block = es.enter_context(nc.Block())

        def run(en):
            def body(eng):
                waited = {}
                for o in per[en]:
                    need = {}
                    for d in o.deps:
                        if d.is_dma:
                            sem, val = dmasem[id(d.semkey)], d.semval
                        else:
                            if d.eng == en == "pe" and not o.is_dma:
                                continue
                            sem, val = engsem[d.eng], d.ticket
                        k = id(sem)
                        if need.get(k, (None, 0))[1] < val:
                            need[k] = (sem, val)
                    for k, (sem, val) in need.items():
                        if waited.get(k, 0) >= val:
                            continue
                        waited[k] = val
                        eng.wait_ge(sem, val)
                    if o.fn is None:
                        continue
                    ins = o.fn(eng)
                    if o.is_dma:
                        ins.then_inc(dmasem[id(o.semkey)], 16)
                    elif o.sig:
                        ins.then_inc(engsem[en], 1)
            return body

        block.tensor(run("pe"))
        block.scalar(run("act"))
        block.vector(run("dve"))
        block.gpsimd(run("pool"))
        block.sync(run("sp"))


class Plan:
    def __init__(self):
        self.items = []

    def add(self, name, free_shape, dtype, first, last):
        n = int(np.prod(free_shape)) * (4 if dtype == F32 else 2)
        n = (n + 63) // 64 * 64
        self.items.append([name, tuple(free_shape), dtype, first, last, n, None])

    def _place(self, order):
        placed = []
        for it in order:
            cands = sorted((p[6], p[6] + p[5]) for p in placed if not (p[4] < it[3] or it[4] < p[3]))
            off = 0
            for a, b in cands:
                if off + it[5] <= a:
                    break
                off = max(off, b)
            it[6] = off
            placed.append(it)
        return max(p[6] + p[5] for p in placed)

    def solve(self):
        rng = np.random.RandomState(0)
        keys = [lambda t: (-t[5],), lambda t: (-(t[4] - t[3]), -t[5]), lambda t: (t[3], -t[5]), lambda t: (-t[4], -t[5])]
        best, best_order = None, None
        orders = [sorted(self.items, key=k) for k in keys]
        for _ in range(300):
            base = sorted(self.items, key=lambda t: -t[5] * (1 + 0.5 * rng.rand()) - 20000 * (t[4] - t[3]) * rng.rand())
            orders.append(base)
        for od in orders:
            tot = self._place(od)
            if best is None or tot < best:
                best, best_order = tot, list(od)
        return self._place(best_order)


def build(debug=False, upto=99):
    nc = bass.Bass("TRN2", target_bir_lowering=False)

    def din(name, shape):
        return nc.dram_tensor(name, list(shape), F32, kind="ExternalInput").ap()

    x_d = din("x", [4096, 1024])
    meta_d = din("meta", [16, 1024])
    cos_d = din("cos", [128, 33 * 16])
    sin_d = din("sin", [128, 33 * 16])
    gmix_d = din("gmix", [128, 8])
    gffn_d = din("gffn", [128, 8])
    gq_d = din("gq", [128, 3])
    gkv_d = din("gkv", [128, 2])
    gfin_d = din("gfin", [128, 1024])
    ident_d = din("ident", [128, 128])
    w_in_d = din("w_in", [1024, 4256])
    w_uq_d = din("w_uq", [384, 768])
    w_ukv_d = din("w_ukv", [256, 1024])
    w_na_out_d = din("w_na_out", [512, 1024])
    w_mla_out_d = din("w_mla_out", [512, 1024])
    w_out_d = din("w_out", [1024, 1024])
    w_ff1_d = din("w_ff1", [1024, 4096])
    w_ff2_d = din("w_ff2", [4096, 1024])
    tabI_d = din("tabI", [128, 8 * 5 * 128])
    tabB_d = din("tabB", [128, 2 * 8 * 4 * 128])
    out_d = nc.dram_tensor("out", [2048, 1024], F32, kind="ExternalOutput").ap()
    dbg = {}

    pl = Plan()
    A = pl.add
    A("ident", (128,), BF16, 0, 9); A("ones32", (64,), F32, 0, 9); A("neghalf", (4,), F32, 0, 9)
    A("gmix", (8,), F32, 0, 9); A("gffn", (8,), F32, 0, 9); A("gq", (3,), F32, 0, 9); A("gkv", (2,), F32, 0, 9)
    A("cosT", (33, 16), F32, 0, 9); A("sinT", (33, 16), F32, 0, 9)
    for tg_, (f_, l_) in {"a": (1, 1), "b": (3, 3), "c": (5, 6), "d": (8, 8)}.items():
        for i in range(2):
            A("xg%d@%s" % (i, tg_), (1024,), F32, f_, l_)
            A("xn%d@%s" % (i, tg_), (1024,), BF16, f_, l_)
            A("hnT%d@%s" % (i, tg_), (8, 128), BF16, f_, l_)
        for i in range(4):
            A("st%d@%s" % (i, tg_), (4,), F32, f_, l_)
    A("wA", (8, 1536), BF16, 1, 1)
    A("k_naT", (4, 2320), BF16, 1, 2); A("q_naT", (4, 2048), BF16, 1, 2); A("V_na", (19, 8, 65), BF16, 1, 2)
    A("EBI", (8, 5, 128), BF16, 1, 2); A("EBB", (2, 8, 4, 128), BF16, 1, 2)
    for i in range(2):
        A("E%d" % i, (6, 128), BF16, 2, 2); A("Pm%d" % i, (6, 128), BF16, 2, 2)
        A("o_tm%d" % i, (512,), BF16, 2, 2); A("rec%d" % i, (4,), F32, 2, 2)
    A("o_naT", (4, 2048), BF16, 2, 5)
    A("wB", (8, 672), BF16, 2, 3); A("wuq", (3, 768), BF16, 2, 3); A("wukv", (2, 1024), BF16, 2, 3)
    A("KT", (8, 4112), BF16, 3, 4); A("V_all", (33, 8, 65), BF16, 3, 4); A("QT", (8, 2048), BF16, 3, 5)
    for i in range(2):
        A("ckvn%d" % i, (256,), BF16, 3, 3); A("ckvnT%d" % i, (2, 128), BF16, 3, 3)
        A("cqn%d" % i, (384,), BF16, 3, 3); A("cqnT%d" % i, (3, 128), BF16, 3, 3)
        A("K_tm%d" % i, (8, 96), BF16, 3, 3); A("Q_tm%d" % i, (8, 96), BF16, 3, 3)
        A("rt%d" % i, (6, 8, 16), F32, 3, 3)
        A("stk%d" % i, (4,), F32, 3, 3); A("stq%d" % i, (4,), F32, 3, 3)
    for i in range(2):
        A("P%d" % i, (2, 512), BF16, 4, 4)
        A("sr%d" % i, (512,), F32, 4, 4); A("rb%d" % i, (512,), F32, 4, 4)
    A("o_mlaT", (8, 2048), BF16, 4, 5)
    A("wNA", (4, 1024), BF16, 4, 5); A("wMLA", (8, 1024), BF16, 5, 5)
    for i in range(2):
        A("hnTg%d" % i, (8, 512), BF16, 5, 5); A("sgn%d" % i, (512,), BF16, 5, 5); A("sgm%d" % i, (512,), BF16, 5, 5)
        A("t1_%d" % i, (512,), F32, 5, 5); A("t2_%d" % i, (512,), F32, 5, 5)
    A("mergedT", (8, 2048), BF16, 5, 6); A("wO", (8, 1024), BF16, 5, 6)
    A("h_acc", (16, 1024), F32, 6, 8); A("fnT", (8, 2048), BF16, 6, 7)
    for i in range(2):
        A("w1g%d" % i, (8, 512), BF16, 6 + i, 7); A("w2g%d" % i, (4, 1024), BF16, 6 + i, 7)
        A("r%d" % i, (512,), BF16, 7, 7); A("uT%d" % i, (4, 512), BF16, 7, 7)
        A("ot%d" % i, (1024,), F32, 8, 8)
    A("gfin", (1024,), F32, 7, 8); A("fst", (48,), F32, 8, 8)
    total = pl.solve()
    assert total <= 212000, total

    es = ExitStack()
    with es:
        arena = es.enter_context(nc.sbuf_tensor("arena", [128, total // 2], BF16))
        ps = es.enter_context(nc.psum_tensor("ps", [128, 4096], F32))
        T = {}
        for name, shp, dt_, f, l, n, off in pl.items:
            ne = int(np.prod(shp))
            if dt_ == F32:
                v = arena[:, off // 2: off // 2 + 2 * ne].bitcast(F32)
            else:
                v = arena[:, off // 2: off // 2 + ne]
            if len(shp) == 2:
                v = v.rearrange("p (a b) -> p a b", b=shp[1])
            elif len(shp) == 3:
                v = v.rearrange("p (a b c) -> p a b c", b=shp[1], c=shp[2])
            elif len(shp) == 4:
                v = v.rearrange("p (a b c d) -> p a b c d", b=shp[1], c=shp[2], d=shp[3])
            T[name] = v

        T["wG"] = T["QT"]
        tag = ["a"]

        def W(n):
            return T[n + "@" + tag[0]]

        p = Prog(nc)
        bankB = [Buf(excl=True) for _ in range(8)]

        def bank(i):
            return ps[:, i * 512:(i + 1) * 512]

        def bank_bf(i):
            return ps[:, i * 512:(i + 1) * 512].bitcast(BF16).rearrange("p (c n) -> p c n", n=128)

        def dma_sp(out, in_, reads, writes, semkey=None):
            p.op("sp", lambda e: e.dma_start(out=out, in_=in_), reads, writes, dma=True, semkey=semkey)

        def dma_cast(out, in_, writes):
            p.op("pool", lambda e: e.dma_start(out=out, in_=in_), [], writes, dma=True)

        def mm(out, lhsT, rhs, start, stop, reads, writes):
            p.op("pe", lambda e: e.matmul(out, lhsT=lhsT, rhs=rhs, start=start, stop=stop), reads, writes)

        def tr(out, in_, np_, reads, writes):
            idn = T["ident"][0:np_, 0:np_]
            p.op("pe", lambda e: e.transpose(out, in_, idn), reads + [identB], writes)

        def act(out, in_, func, reads, writes, scale=None, accum=None):
            kw = {}
            if scale is not None:
                kw["scale"] = scale
            if accum is not None:
                kw["accum_out"] = accum
            p.op("act", lambda e: e.activation(out=out, in_=in_, func=func, **kw), reads, writes)

        def tt(eng, out, in0, in1, op, reads, writes):
            p.op(eng, lambda e: e.tensor_tensor(out=out, in0=in0, in1=in1, op=op), reads, writes)

        def ts(eng, out, in0, s1, s2, op0, op1, reads, writes):
            if op1 is None:
                p.op(eng, lambda e: e.tensor_scalar(out=out, in0=in0, scalar1=s1, scalar2=None, op0=op0), reads, writes)
            else:
                p.op(eng, lambda e: e.tensor_scalar(out=out, in0=in0, scalar1=s1, scalar2=s2, op0=op0, op1=op1), reads, writes)

        def cp(eng, out, in_, reads, writes):
            if eng == "act":
                act(out, in_, AF.Copy, reads, writes)
            else:
                p.op(eng, lambda e: e.tensor_copy(out=out, in_=in_), reads, writes)

        def memset(eng, ap, val, writes):
            p.op(eng, lambda e: e.memset(ap, val), [], writes)

        def recip(out, in_, reads, writes):
            p.op("dve", lambda e: e.reciprocal(out=out, in_=in_), reads, writes)

        def dump(name, ap, shape, dt_, reads):
            if not debug:
                return
            d = nc.dram_tensor("dbg_" + name, list(shape), dt_, kind="ExternalOutput").ap()
            b = Buf()
            dbg[name] = b
            dma_sp(d, ap, reads, [b])

        def wview(d, p_=128):
            return d.rearrange("(k p) n -> p k n", p=p_)

        identB, constB = Buf(), Buf()
        dma_cast(T["ident"], ident_d, [identB])
        cB = [Buf() for _ in range(6)]
        dma_sp(T["gmix"], gmix_d, [], [cB[0]]); dma_sp(T["gffn"], gffn_d, [], [cB[1]])
        dma_sp(T["gq"], gq_d, [], [cB[2]]); dma_sp(T["gkv"], gkv_d, [], [cB[3]])
        dma_sp(T["cosT"], cos_d.rearrange("p (t f) -> p t f", f=16), [], [cB[4]])
        dma_sp(T["sinT"], sin_d.rearrange("p (t f) -> p t f", f=16), [], [cB[5]])
        memset("dve", T["ones32"], 1.0, [constB])
        memset("dve", T["neghalf"], -0.5, [constB])
        gainB = {"gmix": cB[0], "gffn": cB[1], "gq": cB[2], "gkv": cB[3]}
        csB = [cB[4], cB[5]]

        rot = {"mm": 0, "pt": 0, "x": 0}

        def next_bank(lo=0, hi=6):
            i = lo + rot["mm"] % (hi - lo)
            rot["mm"] += 1
            return i

        def next_pt():
            i = 6 + rot["pt"] % 2
            rot["pt"] += 1
            return i

        xgB = [Buf(), Buf()]; xnB = [Buf(), Buf()]; hnTB = [Buf(), Buf()]
        stB = [[Buf(), Buf(), Buf()] for _ in range(4)]
        strot = [0]

        def rstd_of(ss_src, ss_srcB, np_, width, nfeat, st, sB, junk, junkB):
            act(junk, ss_src, AF.Square, [ss_srcB], [sB[0], junkB], accum=st[0:np_, 0:1])
            ts("dve", st[0:np_, 1:2], st[0:np_, 0:1], 1.0 / nfeat, EPS, ALU.mult, ALU.add, [sB[0]], [sB[1]])
            tt("pool", st[0:np_, 2:3], st[0:np_, 1:2], T["neghalf"][0:np_, 0:1], ALU.pow, [sB[1], constB], [sB[2]])

        def norm_T(src, srcB, np_, gain, outT, outTB, ntok0=0):
            i = strot[0]; strot[0] += 1
            st, sB = W("st%d" % (i % 4)), stB[i % 4]
            s2 = i % 2
            rstd_of(src, srcB, np_, 1024, 1024, st, sB, W("xn%d" % s2)[0:np_, :], xnB[s2])
            act(W("xn%d" % s2)[0:np_, :], src, AF.Copy, [srcB, sB[2]], [xnB[s2]], scale=st[0:np_, 2:3])
            pb = next_pt()
            ptv = bank_bf(pb)
            for c in range(8):
                tr(ptv[:, c, 0:np_], W("xn%d" % s2)[0:np_, c * 128:(c + 1) * 128], np_, [xnB[s2]], [bankB[pb]])
            g = T[gain][:, 0:8].unsqueeze(2).broadcast_to([128, 8, np_])
            tt("dve", outT[:, :, ntok0:ntok0 + np_], ptv[:, :, 0:np_], g, ALU.mult, [bankB[pb], gainB[gain]], [outTB])

        def xload(src_rows, np_):
            i = rot["x"]; rot["x"] += 1
            s = i % 2
            dma_sp(W("xg%d" % s)[0:np_, :], src_rows, [], [xgB[s]])
            return W("xg%d" % s)[0:np_, :], xgB[s]

        def xpass(lt):
            if lt == 32:
                src, np_ = meta_d, 16
            else:
                src, np_ = x_d[lt * 128:(lt + 1) * 128, :], 128
            xa, xb = xload(src, np_)
            s = (rot["x"] - 1) % 2
            norm_T(xa, xb, np_, "gmix", W("hnT%d" % s), hnTB[s])
            return W("hnT%d" % s), hnTB[s], np_

        p.enabled = upto >= 1
        wAB, wBB, wuqB, wukvB = Buf(), Buf(), Buf(), Buf()
        dma_cast(T["wA"], wview(w_in_d)[:, :, 0:1536], [wAB])
        tabB_ = [Buf(), Buf()]
        dma_cast(T["EBI"].rearrange("p h q n -> p (h q n)").rearrange("p (a n) -> p a n", n=1024),
                 tabI_d.rearrange("p (a n) -> p a n", n=1024), [tabB_[0]])
        dma_cast(T["EBB"].rearrange("p m h q n -> p (m h q n)").rearrange("p (a n) -> p a n", n=2048),
                 tabB_d.rearrange("p (a n) -> p a n", n=2048), [tabB_[1]])
        vnaB = Buf()
        memset("pool", T["V_na"][:, :, :, 64:65], 1.0, [vnaB])
        na_tiles = list(range(18)) + [32]
        resB = Buf()

        def p1_project(lt, hT, hB, np_):
            kt = 18 if lt == 32 else lt
            tok0 = kt * 128
            b = next_bank()
            for c in range(4):
                for k in range(8):
                    mm(bank(b)[:, c * 128:c * 128 + np_], T["wA"][:, k, 512 + c * 128:512 + (c + 1) * 128], hT[:, k, 0:np_],
                       k == 0, k == 7, [wAB, hB], [bankB[b]])
            cp("act", T["k_naT"][:, :, tok0:tok0 + np_], bank(b).rearrange("p (c n) -> p c n", n=128)[:, :, 0:np_], [bankB[b]], [])
            if lt < 16:
                b = next_bank()
                for c in range(4):
                    for k in range(8):
                        mm(bank(b)[:, c * 128:(c + 1) * 128], T["wA"][:, k, c * 128:(c + 1) * 128], hT[:, k, :],
                           k == 0, k == 7, [wAB, hB], [bankB[b]])
                cp("act", T["q_naT"][:, :, tok0:tok0 + 128], bank(b).rearrange("p (c n) -> p c n", n=128), [bankB[b]], [])
            b = next_bank()
            for k in range(8):
                mm(bank(b)[0:np_, :], hT[:, k, 0:np_], T["wA"][:, k, 1024:1536], k == 0, k == 7, [wAB, hB], [bankB[b]])
            cp("dve", T["V_na"][0:np_, kt, :, 0:64], bank(b)[0:np_, :].rearrange("p (h d) -> p h d", d=64), [bankB[b], vnaB], [])

        cur = xpass(na_tiles[0])
        for i, lt in enumerate(na_tiles):
            nxt = xpass(na_tiles[i + 1]) if i + 1 < len(na_tiles) else None
            p1_project(lt, *cur)
            cur = nxt
        act(T["EBI"], T["EBI"], AF.Exp, [tabB_[0]], [tabB_[0]])
        act(T["EBB"], T["EBB"], AF.Exp, [tabB_[1]], [tabB_[1]])
        dump("k_naT", T["k_naT"], [128, 4, 2320], BF16, [])
        dump("q_naT", T["q_naT"], [128, 4, 2048], BF16, [])
        p.barrier()

        p.enabled = upto >= 2
        dma_cast(T["wB"], wview(w_in_d)[:, :, 1536:2208], [wBB])
        dma_cast(T["wuq"], wview(w_uq_d), [wuqB])
        dma_cast(T["wukv"], wview(w_ukv_d), [wukvB])
        EB_ = [Buf(), Buf()]; EB4_ = [Buf(), Buf()]; EBm_ = [Buf(), Buf()]; PmB = [Buf(), Buf()]; otmB = [Buf(), Buf()]; recB = [Buf(), Buf()]
        units = [(m, h) for m in range(16) for h in range(8)]

        def na_cfg(m):
            return (4, 0) if m < 2 else (5, m - 2)

        def na_qk(i, m, h):
            npairs, lp0 = na_cfg(m)
            hc, hp = h // 2, (h % 2) * 64
            s = i % 2
            Sv = ps[:, s * 1024:(s + 1) * 1024].rearrange("p (c n) -> p c n", n=128)
            q = T["q_naT"][hp:hp + 64, hc, m * 128:(m + 1) * 128]
            for pp in range(npairs):
                bb = bankB[2 * s + (pp // 4)]
                mm(Sv[:, pp, :], T["k_naT"][hp:hp + 64, hc, (lp0 + pp) * 128:(lp0 + pp + 1) * 128], q, True, True, [], [bb])
            mm(Sv[0:16, 5, :], T["k_naT"][hp:hp + 64, hc, 2304:2320], q, True, True, [], [bankB[2 * s + 1]])

        def na_rest(i, m, h):
            npairs, lp0 = na_cfg(m)
            s = i % 2
            Sv = ps[:, s * 1024:(s + 1) * 1024].rearrange("p (c n) -> p c n", n=128)
            E, Pm = T["E%d" % s], T["Pm%d" % s]
            sb = [bankB[2 * s], bankB[2 * s + 1]]
            act(E[:, 0:4, :], Sv[:, 0:4, :], AF.Exp, [sb[0]], [EB_[s]], scale=0.125)
            if npairs == 5:
                act(E[:, 4:5, :], Sv[:, 4:5, :], AF.Exp, [sb[1]], [EB4_[s]], scale=0.125)
            act(E[0:16, 5, :], Sv[0:16, 5, :], AF.Exp, [sb[1]], [EBm_[s]], scale=0.125)
            tab = T["EBB"][:, m, h, 0:4, :] if m < 2 else T["EBI"][:, h, :, :]
            tt("dve", Pm[:, 0:npairs, :], E[:, 0:npairs, :], tab, ALU.mult, [EB_[s], EB4_[s]], [PmB[s]])
            ob = 4 + s
            O = bank(ob)
            for pp in range(npairs):
                mm(O[:, 0:65], Pm[:, pp, :], T["V_na"][:, lp0 + pp, h, :], pp == 0, False, [PmB[s]], [bankB[ob]])
            mm(O[:, 0:65], E[0:16, 5, :], T["V_na"][0:16, 18, h, :], False, True, [EBm_[s]], [bankB[ob]])
            rec = T["rec%d" % s]
            recip(rec[:, 0:1], O[:, 64:65], [bankB[ob]], [recB[s]])
            om = m % 2
            ts("dve", T["o_tm%d" % om][:, h * 64:(h + 1) * 64], O[:, 0:64], rec[:, 0:1], None, ALU.mult, None,
               [bankB[ob], recB[s]], [otmB[om]])
            if h == 7:
                pb = next_pt()
                ptv = bank_bf(pb)
                for c in range(4):
                    tr(ptv[:, c, :], T["o_tm%d" % om][:, c * 128:(c + 1) * 128], 128, [otmB[om]], [bankB[pb]])
                cp("act", T["o_naT"][:, :, m * 128:(m + 1) * 128], ptv[:, 0:4, :], [bankB[pb]], [])

        na_qk(0, *units[0])
        for i, (m, h) in enumerate(units):
            if i + 1 < len(units):
                na_qk(i + 1, *units[i + 1])
            na_rest(i, m, h)
        dump("o_naT", T["o_naT"], [128, 4, 2048], BF16, [])
        p.barrier()

        p.enabled = upto >= 3
        tag[0] = "b"
        vallB = Buf()
        memset("pool", T["V_all"][:, :, :, 64:65], 1.0, [vallB])
        tB = {k: [Buf(), Buf()] for k in ["ckvn", "ckvnT", "cqn", "cqnT", "K_tm", "Q_tm", "rtk", "rtq", "krot"]}
        stkB = [[Buf(), Buf(), Buf()] for _ in range(2)]
        stqB = [[Buf(), Buf(), Buf()] for _ in range(2)]

        def rope(x1, x2, cosb, sinb, o1, o2, rt, srcB, rtB, outB, np_):
            rd = [srcB] + csB
            tt("dve", rt[0], x1, cosb, ALU.mult, rd, [rtB])
            tt("dve", rt[1], x2, sinb, ALU.mult, rd, [rtB])
            tt("dve", rt[2], x2, cosb, ALU.mult, rd, [rtB])
            tt("dve", rt[3], x1, sinb, ALU.mult, rd, [rtB])
            tt("dve", o1, rt[0], rt[1], ALU.subtract, [rtB], [outB])
            tt("dve", o2, rt[2], rt[3], ALU.add, [rtB], [outB])

        def p3_f1(lt, hT, hB, np_, i):
            kt, s = lt, i % 2
            cosr, sinr = T["cosT"][0:np_, kt, :], T["sinT"][0:np_, kt, :]
            b = 4
            for k in range(8):
                mm(bank(b)[0:np_, 0:288], hT[:, k, 0:np_], T["wB"][:, k, 384:672], k == 0, k == 7, [wBB, hB], [bankB[b]])
            if lt < 16:
                for k in range(8):
                    mm(bank(5)[:, 0:384], hT[:, k, :], T["wB"][:, k, 0:384], k == 0, k == 7, [wBB, hB], [bankB[5]])
            st, sB = T["stk%d" % s], stkB[s]
            rstd_of(bank(b)[0:np_, 0:256], bankB[b], np_, 256, 256, st, sB, T["ckvn%d" % s][0:np_, :], tB["ckvn"][s])
            ts("dve", T["ckvn%d" % s][0:np_, :], bank(b)[0:np_, 0:256], st[0:np_, 2:3], None, ALU.mult, None,
               [bankB[b], sB[2]], [tB["ckvn"][s]])
            rt = T["rt%d" % s]
            Ktm = T["K_tm%d" % s]
            rope(bank(b)[0:np_, 256:272], bank(b)[0:np_, 272:288], cosr, sinr,
                 rt[0:np_, 4, 0, :], rt[0:np_, 5, 0, :], [rt[0:np_, j, 0, :] for j in range(4)],
                 bankB[b], tB["rtk"][s], tB["krot"][s], np_)
            cp("dve", Ktm[0:np_, :, 64:80], rt[0:np_, 4, 0:1, :].broadcast_to([np_, 8, 16]), [tB["krot"][s]], [tB["K_tm"][s]])
            cp("dve", Ktm[0:np_, :, 80:96], rt[0:np_, 5, 0:1, :].broadcast_to([np_, 8, 16]), [tB["krot"][s]], [tB["K_tm"][s]])
            if lt < 16:
                st, sB = T["stq%d" % s], stqB[s]
                rstd_of(bank(5)[:, 0:384], bankB[5], 128, 384, 384, st, sB, T["cqn%d" % s], tB["cqn"][s])
                ts("dve", T["cqn%d" % s], bank(5)[:, 0:384], st[:, 2:3], None, ALU.mult, None, [bankB[5], sB[2]], [tB["cqn"][s]])

        def p3_f2(lt, np_, i):
            s = i % 2
            pb = next_pt()
            ptv = bank_bf(pb)
            for c in range(2):
                tr(ptv[:, c, 0:np_], T["ckvn%d" % s][0:np_, c * 128:(c + 1) * 128], np_, [tB["ckvn"][s]], [bankB[pb]])
            if lt < 16:
                for c in range(3):
                    tr(ptv[:, 2 + c, :], T["cqn%d" % s][:, c * 128:(c + 1) * 128], 128, [tB["cqn"][s]], [bankB[pb]])
            g = T["gkv"][:, 0:2].unsqueeze(2).broadcast_to([128, 2, np_])
            tt("dve", T["ckvnT%d" % s][:, :, 0:np_], ptv[:, 0:2, 0:np_], g, ALU.mult, [bankB[pb], gainB["gkv"]], [tB["ckvnT"][s]])
            if lt < 16:
                g = T["gq"][:, 0:3].unsqueeze(2).broadcast_to([128, 3, 128])
                tt("dve", T["cqnT%d" % s], ptv[:, 2:5, :], g, ALU.mult, [bankB[pb], gainB["gq"]], [tB["cqnT"][s]])

        def p3_b1(lt, np_, i):
            kt, s = lt, i % 2
            cosr, sinr = T["cosT"][0:np_, kt, :], T["sinT"][0:np_, kt, :]
            Ktm, rt = T["K_tm%d" % s], T["rt%d" % s]
            for half in range(2):
                for k in range(2):
                    mm(bank(half)[0:np_, :], T["ckvnT%d" % s][:, k, 0:np_], T["wukv"][:, k, half * 512:(half + 1) * 512],
                       k == 0, k == 1, [wukvB, tB["ckvnT"][s]], [bankB[half]])
            if lt < 16:
                for half, (c0, c1) in enumerate([(0, 384), (384, 768)]):
                    for k in range(3):
                        mm(bank(2 + half)[:, 0:384], T["cqnT%d" % s][:, k, :], T["wuq"][:, k, c0:c1],
                           k == 0, k == 2, [wuqB, tB["cqnT"][s]], [bankB[2 + half]])
            for half in range(2):
                kvh = bank(half)[0:np_, :].rearrange("p (h d) -> p h d", d=128)
                hs = slice(half * 4, (half + 1) * 4)
                cp("act", Ktm[0:np_, hs, 0:64], kvh[:, :, 0:64], [bankB[half]], [tB["K_tm"][s]])
                cp("act", T["V_all"][0:np_, kt, hs, 0:64], kvh[:, :, 64:128], [bankB[half], vallB], [])
            if lt < 16:
                Qtm = T["Q_tm%d" % s]
                cosb = cosr.unsqueeze(1).broadcast_to([128, 4, 16])
                sinb = sinr.unsqueeze(1).broadcast_to([128, 4, 16])
                for half in range(2):
                    q3 = bank(2 + half)[:, 0:384].rearrange("p (h d) -> p h d", d=96)
                    hs = slice(half * 4, (half + 1) * 4)
                    qb_ = [bankB[2 + half]]
                    rd = qb_ + csB
                    tt("dve", rt[:, 0, hs], q3[:, :, 64:80], cosb, ALU.mult, rd, [tB["rtk"][s]])
                    tt("dve", rt[:, 1, hs], q3[:, :, 80:96], sinb, ALU.mult, rd, [tB["rtk"][s]])
                    tt("dve", rt[:, 2, hs], q3[:, :, 80:96], cosb, ALU.mult, rd, [tB["rtk"][s]])
                    tt("dve", rt[:, 3, hs], q3[:, :, 64:80], sinb, ALU.mult, rd, [tB["rtk"][s]])
                    cp("dve", Qtm[:, hs, 0:64], q3[:, :, 0:64], qb_, [tB["Q_tm"][s]])
                tt("pool", Qtm[:, :, 64:80], rt[:, 0], rt[:, 1], ALU.subtract, [tB["rtk"][s]], [tB["Q_tm"][s]])
                tt("pool", Qtm[:, :, 80:96], rt[:, 2], rt[:, 3], ALU.add, [tB["rtk"][s]], [tB["Q_tm"][s]])

        def p3_b2(lt, np_, i):
            kt, s = lt, i % 2
            tok0 = kt * 128
            Ktm = T["K_tm%d" % s]
            pb = next_pt()
            ptv = bank_bf(pb)
            for h in range(8):
                tr(ptv[0:96, h, 0:np_], Ktm[0:np_, h, :], np_, [tB["K_tm"][s]], [bankB[pb]])
            cp("act", T["KT"][0:96, :, tok0:tok0 + np_], ptv[0:96, :, 0:np_], [bankB[pb]], [])
            if lt < 16:
                Qtm = T["Q_tm%d" % s]
                pb = next_pt()
                ptv = bank_bf(pb)
                for h in range(8):
                    tr(ptv[0:96, h, :], Qtm[:, h, :], 128, [tB["Q_tm"][s]], [bankB[pb]])
                cp("act", T["QT"][0:96, :, tok0:tok0 + 128], ptv[0:96, :, :], [bankB[pb]], [])

        mla_tiles = list(range(33))
        nt_ = len(mla_tiles)
        xs = {0: xpass(mla_tiles[0]), 1: xpass(mla_tiles[1])}
        p3_f1(mla_tiles[0], *xs[0], 0)
        p3_f2(mla_tiles[0], xs[0][2], 0)
        for i, lt in enumerate(mla_tiles):
            if i + 2 < nt_:
                xs[i + 2] = xpass(mla_tiles[i + 2])
            if i + 1 < nt_:
                p3_f1(mla_tiles[i + 1], *xs[i + 1], i + 1)
            p3_b1(lt, xs[i][2], i)
            if i + 1 < nt_:
                p3_f2(mla_tiles[i + 1], xs[i + 1][2], i + 1)
            p3_b2(lt, xs[i][2], i)
        dump("KT", T["KT"][0:96], [96, 8, 4112], BF16, [])
        dump("QT", T["QT"][0:96], [96, 8, 2048], BF16, [])
        dump("V_all", T["V_all"], [128, 33, 8, 65], BF16, [])
        p.barrier()

        p.enabled = upto >= 4
        wNAB, wMLAB, wOB = Buf(), Buf(), Buf()
        wGB = [Buf() for _ in range(4)]
        QTB = [Buf() for _ in range(4)]
        dma_cast(T["wNA"], wview(w_na_out_d), [wNAB])

        def load_wG(j):
            dma_cast(T["wG"][:, :, j * 512:(j + 1) * 512], wview(w_in_d)[:, :, 2208 + j * 512:2208 + (j + 1) * 512], [wGB[j], QTB[j]])

        PB = [[Buf(), Buf()], [Buf(), Buf()]]; srB = [Buf(), Buf()]; rbB = [Buf(), Buf()]
        SC = float(96 ** -0.5)
        steps = []
        for u, (qb, h) in enumerate([(qb, h) for qb in range(4) for h in range(8)]):
            for sidx in range(17):
                steps.append((u, qb, h, sidx))

        def mla_qk(i, u, qb, h, sidx):
            s = i % 2
            q = T["QT"][0:96, h, qb * 512:(qb + 1) * 512]
            if sidx < 16:
                for j in range(2):
                    kb = 2 * sidx + j
                    mm(bank(2 * s + j), T["KT"][0:96, h, kb * 128:(kb + 1) * 128], q, True, True, [QTB[qb]], [bankB[2 * s + j]])
            else:
                mm(bank(2 * s)[0:16, :], T["KT"][0:96, h, 4096:4112], q, True, True, [QTB[qb]], [bankB[2 * s]])

        def mla_rest(i, u, qb, h, sidx):
            s = i % 2
            Pt = T["P%d" % s]
            ob = 4 + u % 2
            po = bank(ob)
            if sidx < 16:
                for j in range(2):
                    act(Pt[:, j, :], bank(2 * s + j), AF.Exp, [bankB[2 * s + j]], [PB[s][j]], scale=SC)
                for j in range(2):
                    kb = 2 * sidx + j
                    mm(po[0:65, :], T["V_all"][:, kb, h, :], Pt[:, j, :], kb == 0, False, [PB[s][j]], [bankB[ob]])
            else:
                act(Pt[0:16, 0, :], bank(2 * s)[0:16, :], AF.Exp, [bankB[2 * s]], [PB[s][0]], scale=SC)
                mm(po[0:65, :], T["V_all"][0:16, 32, h, :], Pt[0:16, 0, :], False, True, [PB[s][0]], [bankB[ob]])
                us = u % 2
                cp("dve", T["sr%d" % us][64:65, :], po[64:65, :], [bankB[ob]], [srB[us]])

        def mla_norm(u, qb, h):
            us = u % 2
            ob = 4 + us
            pbk = 6 + us
            mm(bank(pbk)[0:64, :], T["ones32"][64:65, 0:64], T["sr%d" % us][64:65, :], True, True, [srB[us], constB], [bankB[pbk]])
            recip(T["rb%d" % us][0:64, :], bank(pbk)[0:64, :], [bankB[pbk]], [rbB[us]])
            tt("dve", T["o_mlaT"][0:64, h, qb * 512:(qb + 1) * 512], bank(ob)[0:64, :], T["rb%d" % us][0:64, :], ALU.mult,
               [bankB[ob], rbB[us]], [])

        mla_qk(0, *steps[0])
        for i, stp in enumerate(steps):
            if i + 1 < len(steps):
                mla_qk(i + 1, *steps[i + 1])
            mla_rest(i, *stp)
            u, qb, h, sidx = stp
            if sidx == 2 and u > 0:
                mla_norm(u - 1, *[(q_, h_) for q_ in range(4) for h_ in range(8)][u - 1])
                if h == 0 and qb >= 1:
                    load_wG(qb - 1)
        mla_norm(31, 3, 7)
        dump("o_mlaT", T["o_mlaT"][0:64], [64, 8, 2048], BF16, [])
        p.barrier()

        p.enabled = upto >= 5
        tag[0] = "c"
        load_wG(3)
        dma_cast(T["wMLA"][0:64], wview(w_mla_out_d, 64), [wMLAB])
        dma_cast(T["wO"], wview(w_out_d), [wOB])
        hgB = [Buf(), Buf()]
        sgnB = [Buf(), Buf()]; sgmB = [Buf(), Buf()]; t1B = [Buf(), Buf()]; t2B = [Buf(), Buf()]

        def p5_x(g):
            s = g % 2
            for j in range(4):
                lt = g * 4 + j
                xa, xb = xload(x_d[lt * 128:(lt + 1) * 128, :], 128)
                norm_T(xa, xb, 128, "gmix", T["hnTg%d" % s], hgB[s], ntok0=j * 128)

        def p5_c(g):
            s = g % 2
            hT, hB_ = T["hnTg%d" % s], hgB[s]
            tk = slice(g * 512, (g + 1) * 512)
            for c8 in range(8):
                r2 = c8 % 2
                bgn, bgm, bpn, bpm = 0 + r2, 2 + r2, 4 + r2, 6 + r2
                for k in range(8):
                    mm(bank(bgn), T["wG"][:, k, c8 * 128:(c8 + 1) * 128], hT[:, k, :], k == 0, k == 7, [wGB[c8 // 4], hB_], [bankB[bgn]])
                for k in range(8):
                    mm(bank(bgm), T["wG"][:, k, (8 + c8) * 128:(9 + c8) * 128], hT[:, k, :], k == 0, k == 7, [wGB[2 + c8 // 4], hB_], [bankB[bgm]])
                act(T["sgn%d" % r2], bank(bgn), AF.Sigmoid, [bankB[bgn]], [sgnB[r2]])
                act(T["sgm%d" % r2], bank(bgm), AF.Sigmoid, [bankB[bgm]], [sgmB[r2]])
                for kc in range(4):
                    mm(bank(bpn), T["wNA"][:, kc, c8 * 128:(c8 + 1) * 128], T["o_naT"][:, kc, tk], kc == 0, kc == 3, [wNAB], [bankB[bpn]])
                for h in range(8):
                    mm(bank(bpm), T["wMLA"][0:64, h, c8 * 128:(c8 + 1) * 128], T["o_mlaT"][0:64, h, tk], h == 0, h == 7, [wMLAB], [bankB[bpm]])
                tt("dve", T["t1_%d" % r2], bank(bpn), T["sgn%d" % r2], ALU.mult, [bankB[bpn], sgnB[r2]], [t1B[r2]])
                tt("dve", T["t2_%d" % r2], bank(bpm), T["sgm%d" % r2], ALU.mult, [bankB[bpm], sgmB[r2]], [t2B[r2]])
                tt("pool", T["mergedT"][:, c8, tk], T["t1_%d" % r2], T["t2_%d" % r2], ALU.add, [t1B[r2], t2B[r2]], [])

        p5_x(0)
        for g in range(4):
            if g + 1 < 4:
                p5_x(g + 1)
            p5_c(g)
        dump("mergedT", T["mergedT"], [128, 8, 2048], BF16, [])
        p.barrier()

        p.enabled = upto >= 6
        w1B = [Buf(), Buf()]; w2B = [Buf(), Buf()]

        def load_ffn(G):
            s = G % 2
            dma_cast(T["w1g%d" % s], wview(w_ff1_d)[:, :, G * 512:(G + 1) * 512], [w1B[s]])
            dma_cast(T["w2g%d" % s], wview(w_ff2_d)[:, G * 4:(G + 1) * 4, :], [w2B[s]])

        load_ffn(0)
        hB = [Buf() for _ in range(16)]
        fnTB = Buf()
        def p6_a(lt):
            xa, xb = xload(x_d[lt * 128:(lt + 1) * 128, :], 128)
            for half in range(2):
                b = next_bank(0, 6)
                for c in range(8):
                    mm(bank(b), T["mergedT"][:, c, lt * 128:(lt + 1) * 128], T["wO"][:, c, half * 512:(half + 1) * 512],
                       c == 0, c == 7, [wOB], [bankB[b]])
                tt("dve", T["h_acc"][:, lt, half * 512:(half + 1) * 512], bank(b), xa[:, half * 512:(half + 1) * 512], ALU.add,
                   [bankB[b], xb], [hB[lt]])

        p6_a(0)
        for lt in range(16):
            if lt + 1 < 16:
                p6_a(lt + 1)
            norm_T(T["h_acc"][:, lt, :], hB[lt], 128, "gffn", T["fnT"], fnTB, ntok0=lt * 128)
        dump("h1", T["h_acc"], [128, 16, 1024], F32, hB)
        p.barrier()

        p.enabled = upto >= 7
        load_ffn(1)
        gfinB = Buf()
        dma_sp(T["gfin"], gfin_d, [], [gfinB])
        rB = [Buf(), Buf()]; uTB = [Buf(), Buf()]
        fsteps = [(G, tg) for G in range(8) for tg in range(4)]

        def ff1(i, G, tg):
            s, ws = i % 2, G % 2
            for f in range(4):
                b = next_bank(0, 4)
                for k in range(8):
                    mm(bank(b), T["w1g%d" % ws][:, k, f * 128:(f + 1) * 128], T["fnT"][:, k, tg * 512:(tg + 1) * 512],
                       k == 0, k == 7, [w1B[ws]], [bankB[b]])
                rs = rot["x"] % 2
                rot["x"] += 1
                act(T["r%d" % rs], bank(b), AF.Relu, [bankB[b]], [rB[rs]])
                tt("pool", T["uT%d" % s][:, f, :], T["r%d" % rs], T["r%d" % rs], ALU.mult, [rB[rs]], [uTB[s]])

        def ff2(i, G, tg):
            s, ws = i % 2, G % 2
            for j in range(4):
                lt = tg * 4 + j
                for half in range(2):
                    b = 4 + rot["pt"] % 4
                    rot["pt"] += 1
                    for f in range(4):
                        mm(bank(b), T["uT%d" % s][:, f, j * 128:(j + 1) * 128], T["w2g%d" % ws][:, f, half * 512:(half + 1) * 512],
                           f == 0, f == 3, [uTB[s], w2B[ws]], [bankB[b]])
                    hv = T["h_acc"][:, lt, half * 512:(half + 1) * 512]
                    tt("dve", hv, bank(b), hv, ALU.add, [bankB[b], hB[lt]], [hB[lt]])

        ff1(0, *fsteps[0])
        for i, (G, tg) in enumerate(fsteps):
            if i + 1 < len(fsteps):
                ff1(i + 1, *fsteps[i + 1])
            ff2(i, G, tg)
            if tg == 3 and G + 2 < 8:
                load_ffn(G + 2)
        p.barrier()

        p.enabled = upto >= 8
        tag[0] = "d"
        otB = [Buf(), Buf()]
        outB = []
        fst = T["fst"]
        fsB = [Buf(), Buf(), Buf()]
        for lt in range(16):
            s = lt % 2
            act(T["ot%d" % s], T["h_acc"][:, lt, :], AF.Square, [hB[lt]], [fsB[0], otB[s]], accum=fst[:, lt:lt + 1])
        ts("dve", fst[:, 16:32], fst[:, 0:16], 1.0 / 1024, EPS, ALU.mult, ALU.add, [fsB[0]], [fsB[1]])
        tt("pool", fst[:, 32:48], fst[:, 16:32], T["neghalf"][:, 0:1].broadcast_to([128, 16]), ALU.pow, [fsB[1], constB], [fsB[2]])
        for lt in range(16):
            s = lt % 2
            ot = T["ot%d" % s]
            p.op("dve", (lambda ot, hv, sc: (lambda e: e.scalar_tensor_tensor(out=ot, in0=hv, scalar=sc, in1=T["gfin"], op0=ALU.mult, op1=ALU.mult)))(ot, T["h_acc"][:, lt, :], fst[:, 32 + lt:33 + lt]),
                 [hB[lt], fsB[2], gfinB], [otB[s]])
            ob_ = Buf()
            outB.append(ob_)
            dma_sp(out_d[lt * 128:(lt + 1) * 128, :], ot, [otB[s]], [ob_], semkey=otB[s])
        p.enabled = True
        p.op("sp", None, outB + list(dbg.values()))
        p.emit(es)
    return nc


def _perm(half):
    t = np.arange(4096)
    if half == 0:
        return t
    r, c = t // 64, t % 64
    return (63 - r) * 64 + c


def _na_tables(rpb, half):
    a = np.arange(2)[:, None, None, None]
    kc = np.arange(64)[None, :, None, None]
    i = np.arange(2)[None, None, :, None]
    c = np.arange(64)[None, None, None, :]

    def tab(m, pairs):
        out = np.full((8, len(pairs), 2, 64, 2, 64), NEG, np.float32)
        for pi, lp in enumerate(pairs):
            rk_l = 2 * lp + a
            rq_l = 2 * m + i
            rk = rk_l if half == 0 else 63 - rk_l
            rq = rq_l if half == 0 else 63 - rq_l
            rs = np.clip(rq - 4, 0, 56)
            rvalid = (rk >= rs) & (rk <= rs + 7)
            cs = np.clip(c - 8, 0, 48)
            cvalid = (kc >= cs) & (kc < cs + 16)
            valid = np.broadcast_to(rvalid & cvalid, (2, 64, 2, 64))
            dr = np.clip(rk - rq + 7, 0, 14)
            dc = np.clip(kc - c + 15, 0, 30)
            dr_b = np.broadcast_to(dr, (2, 64, 2, 64))
            dc_b = np.broadcast_to(dc, (2, 64, 2, 64))
            vals = rpb[:, dr_b, dc_b]
            out[:, pi] = np.where(valid[None], vals, np.float32(NEG))
        return out.reshape(8, len(pairs), 128, 128).transpose(2, 0, 1, 3)

    tabI = tab(4, [2, 3, 4, 5, 6])
    tabB = np.stack([tab(0, [0, 1, 2, 3]), tab(1, [0, 1, 2, 3])], axis=1)
    return np.ascontiguousarray(tabI).reshape(128, -1), np.ascontiguousarray(tabB).reshape(128, -1)


_CACHE = {}


def kernel(x, meta, norm_mix, w_in, na_rpb, mla_q_norm, w_uq, mla_kv_norm, w_ukv,
           w_na_out, w_mla_out, w_out, norm_ffn, w_ff1, w_ff2, norm_final, _debug=False, _cores=None, _upto=99):
    f32 = np.float32
    x = np.asarray(x, f32)
    col = lambda v, n: np.ascontiguousarray(np.asarray(v, f32).reshape(n, 128).T)
    inv_freq = (1.0 / (f32(10000.0) ** (np.arange(0, 32, 2, dtype=f32) / f32(32)))).astype(f32)
    shared = {
        "meta": np.ascontiguousarray(meta, f32),
        "gmix": col(norm_mix[0], 8), "gffn": col(norm_ffn[0], 8),
        "gq": col(mla_q_norm[0], 3), "gkv": col(mla_kv_norm[0], 2),
        "gfin": np.ascontiguousarray(np.broadcast_to(np.asarray(norm_final, f32)[None, :], (128, 1024))),
        "ident": np.eye(128, dtype=f32),
        "w_in": np.ascontiguousarray(w_in[0], f32), "w_uq": np.ascontiguousarray(w_uq[0], f32),
        "w_ukv": np.ascontiguousarray(w_ukv[0], f32), "w_na_out": np.ascontiguousarray(w_na_out[0], f32),
        "w_mla_out": np.ascontiguousarray(w_mla_out[0], f32), "w_out": np.ascontiguousarray(w_out[0], f32),
        "w_ff1": np.ascontiguousarray(w_ff1[0], f32), "w_ff2": np.ascontiguousarray(w_ff2[0], f32),
    }
    per_half = {}
    for half in range(2):
        perm = _perm(half)
        pos = np.concatenate([perm + 16, np.arange(16), np.zeros(112, np.int64)]).astype(f32)
        ang = (pos[:, None] * inv_freq[None, :]).astype(f32)
        cos = np.cos(ang).astype(f32).reshape(33, 128, 16).transpose(1, 0, 2).reshape(128, -1)
        sin = np.sin(ang).astype(f32).reshape(33, 128, 16).transpose(1, 0, 2).reshape(128, -1)
        tabI, tabB = _na_tables(np.asarray(na_rpb[0], f32), half)
        per_half[half] = dict(perm=perm, cos=np.ascontiguousarray(cos), sin=np.ascontiguousarray(sin), tabI=tabI, tabB=tabB)
    cores = list(range(8)) if _cores is None else _cores
    in_maps = []
    for cid in cores:
        b, half = cid // 2, cid % 2
        ph = per_half[half]
        m = dict(shared)
        m["x"] = np.ascontiguousarray(x[b][ph["perm"]])
        m["cos"], m["sin"], m["tabI"], m["tabB"] = ph["cos"], ph["sin"], ph["tabI"], ph["tabB"]
        in_maps.append(m)
    key = (bool(_debug), _upto)
    if key not in _CACHE:
        _CACHE[key] = build(debug=_debug, upto=_upto)
    nc = _CACHE[key]
    res = run_bass_kernel_spmd(nc, in_maps, core_ids=list(range(len(cores))))
    out = np.zeros((4, 4096, 1024), f32)
    for k, cid in enumerate(cores):
        b, half = cid // 2, cid % 2
        out[b, per_half[half]["perm"][:2048]] = res.results[k]["out"]
    if _debug:
        return out, res.results
    return out
```

```python
import os
import numpy as np
from contextlib import ExitStack
P3CUT = int(os.environ.get('P3CUT', '9'))
import concourse.bass as bass
import concourse.mybir as mybir
from concourse.bass_utils import run_bass_kernel_spmd

F32 = mybir.dt.float32
BF16 = mybir.dt.bfloat16
AF = mybir.ActivationFunctionType
ALU = mybir.AluOpType
EPS = 1e-6
NEG = -30000.0


class Buf:
    __slots__ = ("w", "r", "excl")

    def __init__(self, excl=False):
        self.w = None
        self.r = []
        self.excl = excl


class Op:
    __slots__ = ("eng", "fn", "deps", "sig", "ticket", "is_dma", "semkey", "semval", "idx")


class Prog:
    ENGS = ("pe", "act", "dve", "pool", "sp")

    def __init__(self, nc):
        self.nc = nc
        self.ops = []
        self.last = {}
        self.pending_dma = []

    enabled = True

    def op(self, eng, fn, reads=(), writes=(), dma=False, semkey=None):
        if not self.enabled:
            return None
        o = Op()
        o.eng, o.fn, o.is_dma, o.sig, o.ticket, o.semval = eng, fn, dma, False, None, None
        o.idx = len(self.ops)
        deps = {}
        for b in reads:
            if b.w is not None:
                deps[b.w.idx] = b.w
            if b.excl:
                lastr = {}
                for r in b.r:
                    if r.eng != eng and (r.eng not in lastr or lastr[r.eng].idx < r.idx):
                        lastr[r.eng] = r
                for r in lastr.values():
                    deps[r.idx] = r
        for b in writes:
            if b.w is not None:
                deps[b.w.idx] = b.w
            lastr = {}
            for r in b.r:
                if r.is_dma:
                    deps[r.idx] = r
                elif r.eng not in lastr or lastr[r.eng].idx < r.idx:
                    lastr[r.eng] = r
            for r in lastr.values():
                deps[r.idx] = r
        o.deps = list(deps.values())
        for b in reads:
            b.r.append(o)
        for b in writes:
            b.w = o
            b.r = []
        o.semkey = None
        if dma:
            o.semkey = semkey if semkey is not None else (writes[0] if writes else reads[0])
            self.pending_dma.append(o)
        self.last[eng] = o
        self.ops.append(o)
        return o

    def barrier(self):
        deps = [o for o in self.last.values()] + list(self.pending_dma)
        for e in self.ENGS:
            o = Op()
            o.eng, o.fn, o.is_dma, o.sig, o.ticket, o.semval, o.semkey = e, None, False, False, None, None, None
            o.idx = len(self.ops)
            o.deps = list(deps)
            self.ops.append(o)
        self.pending_dma = []
        self.last = {}

    def emit(self, es):
        nc = self.nc
        for o in self.ops:
            for d in o.deps:
                if d.is_dma:
                    continue
                if d.eng == o.eng == "pe" and not o.is_dma:
                    continue
                d.sig = True
        engsem = {e: es.enter_context(nc.semaphore("s_" + e)) for e in self.ENGS}
        cnt = {e: 0 for e in self.ENGS}
        dmasem, dmacnt = {}, {}
        for o in self.ops:
            if o.is_dma:
                k = id(o.semkey)
                if k not in dmasem:
                    dmasem[k] = es.enter_context(nc.semaphore("d%d" % len(dmasem)))
                    dmacnt[k] = 0
                dmacnt[k] += 16
                o.semval = dmacnt[k]
            elif o.sig:
                cnt[o.eng] += 1
                o.ticket = cnt[o.eng]
        per = {e: [o for o in self.ops if o.eng == e] for e in self.ENGS}
        block = es.enter_context(nc.Block())

        def run(en):
            def body(eng):
                waited = {}
                for o in per[en]:
                    need = {}
                    for d in o.deps:
                        if d.is_dma:
                            sem, val = dmasem[id(d.semkey)], d.semval
                        else:
                            if d.eng == en == "pe" and not o.is_dma:
                                continue
                            sem, val = engsem[d.eng], d.ticket
                        k = id(sem)
                        if need.get(k, (None, 0))[1] < val:
                            need[k] = (sem, val)
                    for k, (sem, val) in need.items():
                        if waited.get(k, 0) >= val:
                            continue
                        waited[k] = val
                        eng.wait_ge(sem, val)
                    if o.fn is None:
                        continue
                    ins = o.fn(eng)
                    if o.is_dma:
                        ins.then_inc(dmasem[id(o.semkey)], 16)
                    elif o.sig:
                        ins.then_inc(engsem[en], 1)
            return body

        block.tensor(run("pe"))
        block.scalar(run("act"))
        block.vector(run("dve"))
        block.gpsimd(run("pool"))
        block.sync(run("sp"))


class Plan:
    def __init__(self):
        self.items = []

    def add(self, name, free_shape, dtype, first, last):
        n = int(np.prod(free_shape)) * (4 if dtype == F32 else 2)
        n = (n + 63) // 64 * 64
        self.items.append([name, tuple(free_shape), dtype, first, last, n, None])

    def _place(self, order):
        placed = []
        for it in order:
            cands = sorted((p[6], p[6] + p[5]) for p in placed if not (p[4] < it[3] or it[4] < p[3]))
            off = 0
            for a, b in cands:
                if off + it[5] <= a:
                    break
                off = max(off, b)
            it[6] = off
            placed.append(it)
        return max(p[6] + p[5] for p in placed)

    def solve(self):
        rng = np.random.RandomState(0)
        keys = [lambda t: (-t[5],), lambda t: (-(t[4] - t[3]), -t[5]), lambda t: (t[3], -t[5]), lambda t: (-t[4], -t[5])]
        best, best_order = None, None
        orders = [sorted(self.items, key=k) for k in keys]
        for _ in range(300):
            base = sorted(self.items, key=lambda t: -t[5] * (1 + 0.5 * rng.rand()) - 20000 * (t[4] - t[3]) * rng.rand())
            orders.append(base)
        for od in orders:
            tot = self._place(od)
            if best is None or tot < best:
                best, best_order = tot, list(od)
        return self._place(best_order)


def build(debug=False, upto=99):
    nc = bass.Bass("TRN2", target_bir_lowering=False)

    def din(name, shape):
        return nc.dram_tensor(name, list(shape), F32, kind="ExternalInput").ap()

    x_d = din("x", [4096, 1024])
    meta_d = din("meta", [16, 1024])
    cos_d = din("cos", [128, 33 * 16])
    sin_d = din("sin", [128, 33 * 16])
    gmix_d = din("gmix", [128, 8])
    gffn_d = din("gffn", [128, 8])
    gq_d = din("gq", [128, 3])
    gkv_d = din("gkv", [128, 2])
    gfin_d = din("gfin", [128, 1024])
    ident_d = din("ident", [128, 128])
    w_in_d = din("w_in", [1024, 4256])
    w_uq_d = din("w_uq", [384, 768])
    w_ukv_d = din("w_ukv", [256, 1024])
    w_na_out_d = din("w_na_out", [512, 1024])
    w_mla_out_d = din("w_mla_out", [512, 1024])
    w_out_d = din("w_out", [1024, 1024])
    w_ff1_d = din("w_ff1", [1024, 4096])
    w_ff2_d = din("w_ff2", [4096, 1024])
    tabI_d = din("tabI", [128, 8 * 5 * 128])
    tabB_d = din("tabB", [128, 2 * 8 * 4 * 128])
    out_d = nc.dram_tensor("out", [2048, 1024], F32, kind="ExternalOutput").ap()
    dbg = {}

    pl = Plan()
    A = pl.add
    A("ident", (128,), BF16, 0, 9); A("ones32", (64,), F32, 0, 9); A("neghalf", (4,), F32, 0, 9)
    A("gmix", (8,), F32, 0, 9); A("gffn", (8,), F32, 0, 9); A("gq", (3,), F32, 0, 9); A("gkv", (2,), F32, 0, 9)
    A("cosT", (33, 16), F32, 0, 9); A("sinT", (33, 16), F32, 0, 9)
    for tg_, (f_, l_) in {"a": (1, 1), "b": (3, 3), "c": (5, 6), "d": (8, 8)}.items():
        for i in range(2):
            A("xg%d@%s" % (i, tg_), (1024,), F32, f_, l_)
            A("xn%d@%s" % (i, tg_), (1024,), BF16, f_, l_)
            A("hnT%d@%s" % (i, tg_), (8, 128), BF16, f_, l_)
        for i in range(4):
            A("st%d@%s" % (i, tg_), (4,), F32, f_, l_)
    A("wA", (8, 1536), BF16, 1, 1)
    A("k_naT", (4, 2320), BF16, 1, 2); A("q_naT", (4, 2048), BF16, 1, 2); A("V_na", (19, 8, 65), BF16, 1, 2)
    A("EBI", (8, 5, 128), BF16, 1, 2); A("EBB", (2, 8, 4, 128), BF16, 1, 2)
    for i in range(2):
        A("E%d" % i, (6, 128), BF16, 2, 2); A("Pm%d" % i, (6, 128), BF16, 2, 2)
        A("o_tm%d" % i, (512,), BF16, 2, 2); A("rec%d" % i, (4,), F32, 2, 2)
    A("o_naT", (4, 2048), BF16, 2, 5)
    A("wB", (8, 672), BF16, 2, 3); A("wuq", (3, 768), BF16, 2, 3); A("wukv", (2, 1024), BF16, 2, 3)
    A("KT", (8, 4112), BF16, 3, 4); A("V_all", (33, 8, 65), BF16, 3, 4); A("QT", (8, 2048), BF16, 3, 4)
    for i in range(2):
        A("ckvn%d" % i, (256,), BF16, 3, 3); A("ckvnT%d" % i, (2, 128), BF16, 3, 3)
        A("cqn%d" % i, (384,), BF16, 3, 3); A("cqnT%d" % i, (3, 128), BF16, 3, 3)
        A("K_tm%d" % i, (8, 96), BF16, 3, 3); A("Q_tm%d" % i, (8, 96), BF16, 3, 3)
        A("rt%d" % i, (6, 8, 16), F32, 3, 3)
        A("stk%d" % i, (4,), F32, 3, 3); A("stq%d" % i, (4,), F32, 3, 3)
    for i in range(2):
        A("P%d" % i, (2, 512), BF16, 4, 4)
        A("sr%d" % i, (512,), F32, 4, 4); A("rb%d" % i, (512,), F32, 4, 4)
    A("o_mlaT", (8, 2048), BF16, 4, 5)
    A("wNA", (4, 1024), BF16, 5, 5); A("wG", (8, 2048), BF16, 5, 5); A("wMLA", (8, 1024), BF16, 5, 5)
    for i in range(2):
        A("hnTg%d" % i, (8, 512), BF16, 5, 5); A("sgn%d" % i, (512,), BF16, 5, 5); A("sgm%d" % i, (512,), BF16, 5, 5)
        A("t1_%d" % i, (512,), F32, 5, 5); A("t2_%d" % i, (512,), F32, 5, 5)
    A("mergedT", (8, 2048), BF16, 5, 6); A("wO", (8, 1024), BF16, 5, 6)
    A("h_acc", (16, 1024), F32, 6, 8); A("fnT", (8, 2048), BF16, 6, 7)
    for i in range(2):
        A("w1g%d" % i, (8, 512), BF16, 6 + i, 7); A("w2g%d" % i, (4, 1024), BF16, 6 + i, 7)
        A("r%d" % i, (512,), BF16, 7, 7); A("uT%d" % i, (4, 512), BF16, 7, 7)
        A("ot%d" % i, (1024,), F32, 8, 8)
    A("gfin", (1024,), F32, 7, 8); A("fst", (48,), F32, 8, 8)
    total = pl.solve()
    assert total <= 212000, total

    es = ExitStack()
    with es:
        arena = es.enter_context(nc.sbuf_tensor("arena", [128, total // 2], BF16))
        ps = es.enter_context(nc.psum_tensor("ps", [128, 4096], F32))
        T = {}
        for name, shp, dt_, f, l, n, off in pl.items:
            ne = int(np.prod(shp))
            if dt_ == F32:
                v = arena[:, off // 2: off // 2 + 2 * ne].bitcast(F32)
            else:
                v = arena[:, off // 2: off // 2 + ne]
            if len(shp) == 2:
                v = v.rearrange("p (a b) -> p a b", b=shp[1])
            elif len(shp) == 3:
                v = v.rearrange("p (a b c) -> p a b c", b=shp[1], c=shp[2])
            elif len(shp) == 4:
                v = v.rearrange("p (a b c d) -> p a b c d", b=shp[1], c=shp[2], d=shp[3])
            T[name] = v

        tag = ["a"]

        def W(n):
            return T[n + "@" + tag[0]]

        p = Prog(nc)
        bankB = [Buf(excl=True) for _ in range(8)]

        def bank(i):
            return ps[:, i * 512:(i + 1) * 512]

        def bank_bf(i):
            return ps[:, i * 512:(i + 1) * 512].bitcast(BF16).rearrange("p (c n) -> p c n", n=128)

        def dma_sp(out, in_, reads, writes, semkey=None):
            p.op("sp", lambda e: e.dma_start(out=out, in_=in_), reads, writes, dma=True, semkey=semkey)

        def dma_cast(out, in_, writes):
            p.op("pool", lambda e: e.dma_start(out=out, in_=in_), [], writes, dma=True)

        def mm(out, lhsT, rhs, start, stop, reads, writes):
            p.op("pe", lambda e: e.matmul(out, lhsT=lhsT, rhs=rhs, start=start, stop=stop), reads, writes)

        def tr(out, in_, np_, reads, writes):
            idn = T["ident"][0:np_, 0:np_]
            p.op("pe", lambda e: e.transpose(out, in_, idn), reads + [identB], writes)

        def act(out, in_, func, reads, writes, scale=None, accum=None):
            kw = {}
            if scale is not None:
                kw["scale"] = scale
            if accum is not None:
                kw["accum_out"] = accum
            p.op("act", lambda e: e.activation(out=out, in_=in_, func=func, **kw), reads, writes)

        def tt(eng, out, in0, in1, op, reads, writes):
            p.op(eng, lambda e: e.tensor_tensor(out=out, in0=in0, in1=in1, op=op), reads, writes)

        def ts(eng, out, in0, s1, s2, op0, op1, reads, writes):
            if op1 is None:
                p.op(eng, lambda e: e.tensor_scalar(out=out, in0=in0, scalar1=s1, scalar2=None, op0=op0), reads, writes)
            else:
                p.op(eng, lambda e: e.tensor_scalar(out=out, in0=in0, scalar1=s1, scalar2=s2, op0=op0, op1=op1), reads, writes)

        def cp(eng, out, in_, reads, writes):
            if eng == "act":
                act(out, in_, AF.Copy, reads, writes)
            else:
                p.op(eng, lambda e: e.tensor_copy(out=out, in_=in_), reads, writes)

        def memset(eng, ap, val, writes):
            p.op(eng, lambda e: e.memset(ap, val), [], writes)

        def recip(out, in_, reads, writes):
            p.op("dve", lambda e: e.reciprocal(out=out, in_=in_), reads, writes)

        def dump(name, ap, shape, dt_, reads):
            if not debug:
                return
            d = nc.dram_tensor("dbg_" + name, list(shape), dt_, kind="ExternalOutput").ap()
            b = Buf()
            dbg[name] = b
            dma_sp(d, ap, reads, [b])

        def wview(d, p_=128):
            return d.rearrange("(k p) n -> p k n", p=p_)

        identB, constB = Buf(), Buf()
        dma_cast(T["ident"], ident_d, [identB])
        cB = [Buf() for _ in range(6)]
        dma_sp(T["gmix"], gmix_d, [], [cB[0]]); dma_sp(T["gffn"], gffn_d, [], [cB[1]])
        dma_sp(T["gq"], gq_d, [], [cB[2]]); dma_sp(T["gkv"], gkv_d, [], [cB[3]])
        dma_sp(T["cosT"], cos_d.rearrange("p (t f) -> p t f", f=16), [], [cB[4]])
        dma_sp(T["sinT"], sin_d.rearrange("p (t f) -> p t f", f=16), [], [cB[5]])
        memset("dve", T["ones32"], 1.0, [constB])
        memset("dve", T["neghalf"], -0.5, [constB])
        gainB = {"gmix": cB[0], "gffn": cB[1], "gq": cB[2], "gkv": cB[3]}
        csB = [cB[4], cB[5]]

        rot = {"mm": 0, "pt": 0, "x": 0}

        def next_bank(lo=0, hi=6):
            i = lo + rot["mm"] % (hi - lo)
            rot["mm"] += 1
            return i

        def next_pt():
            i = 6 + rot["pt"] % 2
            rot["pt"] += 1
            return i

        xgB = [Buf(), Buf()]; xnB = [Buf(), Buf()]; hnTB = [Buf(), Buf()]
        stB = [[Buf(), Buf(), Buf()] for _ in range(4)]
        strot = [0]

        def rstd_of(ss_src, ss_srcB, np_, width, nfeat, st, sB, junk, junkB):
            act(junk, ss_src, AF.Square, [ss_srcB], [sB[0], junkB], accum=st[0:np_, 0:1])
            ts("dve", st[0:np_, 1:2], st[0:np_, 0:1], 1.0 / nfeat, EPS, ALU.mult, ALU.add, [sB[0]], [sB[1]])
            tt("pool", st[0:np_, 2:3], st[0:np_, 1:2], T["neghalf"][0:np_, 0:1], ALU.pow, [sB[1], constB], [sB[2]])

        def norm_front(src, srcB, np_):
            i = strot[0]; strot[0] += 1
            st, sB = W("st%d" % (i % 4)), stB[i % 4]
            s2 = i % 2
            rstd_of(src, srcB, np_, 1024, 1024, st, sB, W("xn%d" % s2)[0:np_, :], xnB[s2])
            act(W("xn%d" % s2)[0:np_, :], src, AF.Copy, [srcB, sB[2]], [xnB[s2]], scale=st[0:np_, 2:3])
            return s2

        def norm_back(s2, np_, gain, outT, outTB, ntok0=0):
            pb = next_pt()
            ptv = bank_bf(pb)
            for c in range(8):
                tr(ptv[:, c, 0:np_], W("xn%d" % s2)[0:np_, c * 128:(c + 1) * 128], np_, [xnB[s2]], [bankB[pb]])
            g = T[gain][:, 0:8].unsqueeze(2).broadcast_to([128, 8, np_])
            tt("dve", outT[:, :, ntok0:ntok0 + np_], ptv[:, :, 0:np_], g, ALU.mult, [bankB[pb], gainB[gain]], [outTB])

        def norm_T(src, srcB, np_, gain, outT, outTB, ntok0=0):
            norm_back(norm_front(src, srcB, np_), np_, gain, outT, outTB, ntok0)

        def xload(src_rows, np_):
            i = rot["x"]; rot["x"] += 1
            s = i % 2
            dma_sp(W("xg%d" % s)[0:np_, :], src_rows, [], [xgB[s]])
            return W("xg%d" % s)[0:np_, :], xgB[s]

        def xpass_front(lt):
            if lt == 32:
                src, np_ = meta_d, 16
            else:
                src, np_ = x_d[lt * 128:(lt + 1) * 128, :], 128
            xa, xb = xload(src, np_)
            s = (rot["x"] - 1) % 2
            return (np_, s, norm_front(xa, xb, np_))

        def xpass_back(fr):
            np_, s, s2 = fr
            norm_back(s2, np_, "gmix", W("hnT%d" % s), hnTB[s])
            return W("hnT%d" % s), hnTB[s], np_

        p.enabled = upto >= 1
        wAB, wBB, wuqB, wukvB = Buf(), Buf(), Buf(), Buf()
        dma_cast(T["wA"], wview(w_in_d)[:, :, 0:1536], [wAB])
        tabB_ = [Buf(), Buf()]
        dma_cast(T["EBI"].rearrange("p h q n -> p (h q n)").rearrange("p (a n) -> p a n", n=1024),
                 tabI_d.rearrange("p (a n) -> p a n", n=1024), [tabB_[0]])
        dma_cast(T["EBB"].rearrange("p m h q n -> p (m h q n)").rearrange("p (a n) -> p a n", n=2048),
                 tabB_d.rearrange("p (a n) -> p a n", n=2048), [tabB_[1]])
        vnaB = Buf()
        memset("pool", T["V_na"][:, :, :, 64:65], 1.0, [vnaB])
        na_tiles = list(range(18)) + [32]
        resB = Buf()

        def p1_project(lt, hT, hB, np_):
            kt = 18 if lt == 32 else lt
            tok0 = kt * 128
            b = next_bank()
            for c in range(4):
                for k in range(8):
                    mm(bank(b)[:, c * 128:c * 128 + np_], T["wA"][:, k, 512 + c * 128:512 + (c + 1) * 128], hT[:, k, 0:np_],
                       k == 0, k == 7, [wAB, hB], [bankB[b]])
            cp("act", T["k_naT"][:, :, tok0:tok0 + np_], bank(b).rearrange("p (c n) -> p c n", n=128)[:, :, 0:np_], [bankB[b]], [])
            if lt < 16:
                b = next_bank()
                for c in range(4):
                    for k in range(8):
                        mm(bank(b)[:, c * 128:(c + 1) * 128], T["wA"][:, k, c * 128:(c + 1) * 128], hT[:, k, :],
                           k == 0, k == 7, [wAB, hB], [bankB[b]])
                cp("act", T["q_naT"][:, :, tok0:tok0 + 128], bank(b).rearrange("p (c n) -> p c n", n=128), [bankB[b]], [])
            b = next_bank()
            for k in range(8):
                mm(bank(b)[0:np_, :], hT[:, k, 0:np_], T["wA"][:, k, 1024:1536], k == 0, k == 7, [wAB, hB], [bankB[b]])
            cp("dve", T["V_na"][0:np_, kt, :, 0:64], bank(b)[0:np_, :].rearrange("p (h d) -> p h d", d=64), [bankB[b], vnaB], [])

        nn_ = len(na_tiles)
        fr = {0: xpass_front(na_tiles[0]), 1: xpass_front(na_tiles[1])}
        xs = {0: xpass_back(fr[0])}
        for i, lt in enumerate(na_tiles):
            if i + 2 < nn_:
                fr[i + 2] = xpass_front(na_tiles[i + 2])
            if i + 1 < nn_:
                xs[i + 1] = xpass_back(fr[i + 1])
            p1_project(lt, *xs[i])
        act(T["EBI"], T["EBI"], AF.Exp, [tabB_[0]], [tabB_[0]])
        act(T["EBB"], T["EBB"], AF.Exp, [tabB_[1]], [tabB_[1]])
        dump("k_naT", T["k_naT"], [128, 4, 2320], BF16, [])
        dump("q_naT", T["q_naT"], [128, 4, 2048], BF16, [])
        p.barrier()

        p.enabled = upto >= 2
        dma_cast(T["wB"], wview(w_in_d)[:, :, 1536:2208], [wBB])
        dma_cast(T["wuq"], wview(w_uq_d), [wuqB])
        dma_cast(T["wukv"], wview(w_ukv_d), [wukvB])
        EB_ = [Buf(), Buf()]; EB4_ = [Buf(), Buf()]; EBm_ = [Buf(), Buf()]; PmB = [Buf(), Buf()]; otmB = [Buf(), Buf()]; recB = [Buf(), Buf()]
        units = [(m, h) for m in range(16) for h in range(8)]

        def na_cfg(m):
            return (4, 0) if m < 2 else (5, m - 2)

        def na_qk(i, m, h):
            npairs, lp0 = na_cfg(m)
            hc, hp = h // 2, (h % 2) * 64
            s = i % 2
            Sv = ps[:, s * 1024:(s + 1) * 1024].rearrange("p (c n) -> p c n", n=128)
            q = T["q_naT"][hp:hp + 64, hc, m * 128:(m + 1) * 128]
            for pp in range(npairs):
                bb = bankB[2 * s + (pp // 4)]
                mm(Sv[:, pp, :], T["k_naT"][hp:hp + 64, hc, (lp0 + pp) * 128:(lp0 + pp + 1) * 128], q, True, True, [], [bb])
            mm(Sv[0:16, 5, :], T["k_naT"][hp:hp + 64, hc, 2304:2320], q, True, True, [], [bankB[2 * s + 1]])

        def na_rest(i, m, h):
            npairs, lp0 = na_cfg(m)
            s = i % 2
            Sv = ps[:, s * 1024:(s + 1) * 1024].rearrange("p (c n) -> p c n", n=128)
            E, Pm = T["E%d" % s], T["Pm%d" % s]
            sb = [bankB[2 * s], bankB[2 * s + 1]]
            act(E[:, 0:4, :], Sv[:, 0:4, :], AF.Exp, [sb[0]], [EB_[s]], scale=0.125)
            if npairs == 5:
                act(E[:, 4:5, :], Sv[:, 4:5, :], AF.Exp, [sb[1]], [EB4_[s]], scale=0.125)
            act(E[0:16, 5, :], Sv[0:16, 5, :], AF.Exp, [sb[1]], [EBm_[s]], scale=0.125)
            tab = T["EBB"][:, m, h, 0:4, :] if m < 2 else T["EBI"][:, h, :, :]
            tt("dve", Pm[:, 0:npairs, :], E[:, 0:npairs, :], tab, ALU.mult, [EB_[s], EB4_[s]], [PmB[s]])
            ob = 4 + s
            O = bank(ob)
            for pp in range(npairs):
                mm(O[:, 0:65], Pm[:, pp, :], T["V_na"][:, lp0 + pp, h, :], pp == 0, False, [PmB[s]], [bankB[ob]])
            mm(O[:, 0:65], E[0:16, 5, :], T["V_na"][0:16, 18, h, :], False, True, [EBm_[s]], [bankB[ob]])
            rec = T["rec%d" % s]
            recip(rec[:, 0:1], O[:, 64:65], [bankB[ob]], [recB[s]])
            om = m % 2
            ts("dve", T["o_tm%d" % om][:, h * 64:(h + 1) * 64], O[:, 0:64], rec[:, 0:1], None, ALU.mult, None,
               [bankB[ob], recB[s]], [otmB[om]])
            if h == 7:
                pb = next_pt()
                ptv = bank_bf(pb)
                for c in range(4):
                    tr(ptv[:, c, :], T["o_tm%d" % om][:, c * 128:(c + 1) * 128], 128, [otmB[om]], [bankB[pb]])
                cp("act", T["o_naT"][:, :, m * 128:(m + 1) * 128], ptv[:, 0:4, :], [bankB[pb]], [])

        na_qk(0, *units[0])
        for i, (m, h) in enumerate(units):
            if i + 1 < len(units):
                na_qk(i + 1, *units[i + 1])
            na_rest(i, m, h)
        dump("o_naT", T["o_naT"], [128, 4, 2048], BF16, [])
        p.barrier()

        p.enabled = upto >= 3
        tag[0] = "b"
        vallB = Buf()
        memset("pool", T["V_all"][:, :, :, 64:65], 1.0, [vallB])
        tB = {k: [Buf(), Buf()] for k in ["ckvn", "ckvnT", "cqn", "cqnT", "K_tm", "Q_tm", "rtk", "rtq", "krot"]}
        stkB = [[Buf(), Buf(), Buf()] for _ in range(2)]
        stqB = [[Buf(), Buf(), Buf()] for _ in range(2)]

        def rope(x1, x2, cosb, sinb, o1, o2, rt, srcB, rtB, outB, np_):
            rd = [srcB] + csB
            tt("dve", rt[0], x1, cosb, ALU.mult, rd, [rtB])
            tt("dve", rt[1], x2, sinb, ALU.mult, rd, [rtB])
            tt("dve", rt[2], x2, cosb, ALU.mult, rd, [rtB])
            tt("dve", rt[3], x1, sinb, ALU.mult, rd, [rtB])
            tt("dve", o1, rt[0], rt[1], ALU.subtract, [rtB], [outB])
            tt("dve", o2, rt[2], rt[3], ALU.add, [rtB], [outB])

        def p3_f1(lt, hT, hB, np_, i):
            kt, s = lt, i % 2
            cosr, sinr = T["cosT"][0:np_, kt, :], T["sinT"][0:np_, kt, :]
            b = 4
            for k in range(8):
                mm(bank(b)[0:np_, 0:288], hT[:, k, 0:np_], T["wB"][:, k, 384:672], k == 0, k == 7, [wBB, hB], [bankB[b]])
            if lt < 16:
                for k in range(8):
                    mm(bank(5)[:, 0:384], hT[:, k, :], T["wB"][:, k, 0:384], k == 0, k == 7, [wBB, hB], [bankB[5]])
            st, sB = T["stk%d" % s], stkB[s]
            rstd_of(bank(b)[0:np_, 0:256], bankB[b], np_, 256, 256, st, sB, T["ckvn%d" % s][0:np_, :], tB["ckvn"][s])
            ts("dve", T["ckvn%d" % s][0:np_, :], bank(b)[0:np_, 0:256], st[0:np_, 2:3], None, ALU.mult, None,
               [bankB[b], sB[2]], [tB["ckvn"][s]])
            rt = T["rt%d" % s]
            Ktm = T["K_tm%d" % s]
            rope(bank(b)[0:np_, 256:272], bank(b)[0:np_, 272:288], cosr, sinr,
                 rt[0:np_, 4, 0, :], rt[0:np_, 5, 0, :], [rt[0:np_, j, 0, :] for j in range(4)],
                 bankB[b], tB["rtk"][s], tB["krot"][s], np_)
            cp("dve", Ktm[0:np_, :, 64:80], rt[0:np_, 4, 0:1, :].broadcast_to([np_, 8, 16]), [tB["krot"][s]], [tB["K_tm"][s]])
            cp("dve", Ktm[0:np_, :, 80:96], rt[0:np_, 5, 0:1, :].broadcast_to([np_, 8, 16]), [tB["krot"][s]], [tB["K_tm"][s]])
            if lt < 16:
                st, sB = T["stq%d" % s], stqB[s]
                rstd_of(bank(5)[:, 0:384], bankB[5], 128, 384, 384, st, sB, T["cqn%d" % s], tB["cqn"][s])
                ts("dve", T["cqn%d" % s], bank(5)[:, 0:384], st[:, 2:3], None, ALU.mult, None, [bankB[5], sB[2]], [tB["cqn"][s]])

        def p3_f2(lt, np_, i):
            s = i % 2
            pb = next_pt()
            ptv = bank_bf(pb)
            for c in range(2):
                tr(ptv[:, c, 0:np_], T["ckvn%d" % s][0:np_, c * 128:(c + 1) * 128], np_, [tB["ckvn"][s]], [bankB[pb]])
            if lt < 16:
                for c in range(3):
                    tr(ptv[:, 2 + c, :], T["cqn%d" % s][:, c * 128:(c + 1) * 128], 128, [tB["cqn"][s]], [bankB[pb]])
            g = T["gkv"][:, 0:2].unsqueeze(2).broadcast_to([128, 2, np_])
            tt("dve", T["ckvnT%d" % s][:, :, 0:np_], ptv[:, 0:2, 0:np_], g, ALU.mult, [bankB[pb], gainB["gkv"]], [tB["ckvnT"][s]])
            if lt < 16:
                g = T["gq"][:, 0:3].unsqueeze(2).broadcast_to([128, 3, 128])
                tt("dve", T["cqnT%d" % s], ptv[:, 2:5, :], g, ALU.mult, [bankB[pb], gainB["gq"]], [tB["cqnT"][s]])

        def p3_b1(lt, np_, i):
            kt, s = lt, i % 2
            cosr, sinr = T["cosT"][0:np_, kt, :], T["sinT"][0:np_, kt, :]
            Ktm, rt = T["K_tm%d" % s], T["rt%d" % s]
            for half in range(2):
                for k in range(2):
                    mm(bank(half)[0:np_, :], T["ckvnT%d" % s][:, k, 0:np_], T["wukv"][:, k, half * 512:(half + 1) * 512],
                       k == 0, k == 1, [wukvB, tB["ckvnT"][s]], [bankB[half]])
            if lt < 16:
                for half, (c0, c1) in enumerate([(0, 384), (384, 768)]):
                    for k in range(3):
                        mm(bank(2 + half)[:, 0:384], T["cqnT%d" % s][:, k, :], T["wuq"][:, k, c0:c1],
                           k == 0, k == 2, [wuqB, tB["cqnT"][s]], [bankB[2 + half]])
            for half in range(2):
                kvh = bank(half)[0:np_, :].rearrange("p (h d) -> p h d", d=128)
                hs = slice(half * 4, (half + 1) * 4)
                cp("act", Ktm[0:np_, hs, 0:64], kvh[:, :, 0:64], [bankB[half]], [tB["K_tm"][s]])
                cp("act", T["V_all"][0:np_, kt, hs, 0:64], kvh[:, :, 64:128], [bankB[half], vallB], [])
            if lt < 16:
                Qtm = T["Q_tm%d" % s]
                cosb = cosr.unsqueeze(1).broadcast_to([128, 4, 16])
                sinb = sinr.unsqueeze(1).broadcast_to([128, 4, 16])
                for half in range(2):
                    q3 = bank(2 + half)[:, 0:384].rearrange("p (h d) -> p h d", d=96)
                    hs = slice(half * 4, (half + 1) * 4)
                    qb_ = [bankB[2 + half]]
                    rd = qb_ + csB
                    tt("dve", rt[:, 0, hs], q3[:, :, 64:80], cosb, ALU.mult, rd, [tB["rtk"][s]])
                    tt("dve", rt[:, 1, hs], q3[:, :, 80:96], sinb, ALU.mult, rd, [tB["rtk"][s]])
                    tt("dve", rt[:, 2, hs], q3[:, :, 80:96], cosb, ALU.mult, rd, [tB["rtk"][s]])
                    tt("dve", rt[:, 3, hs], q3[:, :, 64:80], sinb, ALU.mult, rd, [tB["rtk"][s]])
                    cp("dve", Qtm[:, hs, 0:64], q3[:, :, 0:64], qb_, [tB["Q_tm"][s]])
                tt("pool", Qtm[:, :, 64:80], rt[:, 0], rt[:, 1], ALU.subtract, [tB["rtk"][s]], [tB["Q_tm"][s]])
                tt("pool", Qtm[:, :, 80:96], rt[:, 2], rt[:, 3], ALU.add, [tB["rtk"][s]], [tB["Q_tm"][s]])

        def p3_b2(lt, np_, i):
            kt, s = lt, i % 2
            tok0 = kt * 128
            Ktm = T["K_tm%d" % s]
            pb = next_pt()
            ptv = bank_bf(pb)
            for h in range(8):
                tr(ptv[0:96, h, 0:np_], Ktm[0:np_, h, :], np_, [tB["K_tm"][s]], [bankB[pb]])
            cp("act", T["KT"][0:96, :, tok0:tok0 + np_], ptv[0:96, :, 0:np_], [bankB[pb]], [])
            if lt < 16:
                Qtm = T["Q_tm%d" % s]
                pb = next_pt()
                ptv = bank_bf(pb)
                for h in range(8):
                    tr(ptv[0:96, h, :], Qtm[:, h, :], 128, [tB["Q_tm"][s]], [bankB[pb]])
                cp("act", T["QT"][0:96, :, tok0:tok0 + 128], ptv[0:96, :, :], [bankB[pb]], [])

        mla_tiles = list(range(33))
        nt_ = len(mla_tiles)
        fr = {0: xpass_front(mla_tiles[0]), 1: xpass_front(mla_tiles[1])}
        xs = {0: xpass_back(fr[0])}
        fr[2] = xpass_front(mla_tiles[2])
        xs[1] = xpass_back(fr[1])
        p3_f1(mla_tiles[0], *xs[0], 0)
        p3_f2(mla_tiles[0], xs[0][2], 0)
        for i, lt in enumerate(mla_tiles):
            if i + 3 < nt_:
                fr[i + 3] = xpass_front(mla_tiles[i + 3])
            if i + 2 < nt_:
                xs[i + 2] = xpass_back(fr[i + 2])
            if i + 1 < nt_:
                p3_f1(mla_tiles[i + 1], *xs[i + 1], i + 1)
            p3_b1(lt, xs[i][2], i)
            if i + 1 < nt_:
                p3_f2(mla_tiles[i + 1], xs[i + 1][2], i + 1)
            p3_b2(lt, xs[i][2], i)
        dump("KT", T["KT"][0:96], [96, 8, 4112], BF16, [])
        dump("QT", T["QT"][0:96], [96, 8, 2048], BF16, [])
        dump("V_all", T["V_all"], [128, 33, 8, 65], BF16, [])
        p.barrier()

        p.enabled = upto >= 4
        wNAB, wGB, wMLAB, wOB = Buf(), Buf(), Buf(), Buf()
        PB = [[Buf(), Buf()], [Buf(), Buf()]]; srB = [Buf(), Buf()]; rbB = [Buf(), Buf()]
        SC = float(96 ** -0.5)
        steps = []
        for u, (qb, h) in enumerate([(qb, h) for qb in range(4) for h in range(8)]):
            for sidx in range(17):
                steps.append((u, qb, h, sidx))

        def mla_qk(i, u, qb, h, sidx):
            s = i % 2
            q = T["QT"][0:96, h, qb * 512:(qb + 1) * 512]
            if sidx < 16:
                for j in range(2):
                    kb = 2 * sidx + j
                    mm(bank(2 * s + j), T["KT"][0:96, h, kb * 128:(kb + 1) * 128], q, True, True, [], [bankB[2 * s + j]])
            else:
                mm(bank(2 * s)[0:16, :], T["KT"][0:96, h, 4096:4112], q, True, True, [], [bankB[2 * s]])

        def mla_rest(i, u, qb, h, sidx):
            s = i % 2
            Pt = T["P%d" % s]
            ob = 4 + u % 2
            po = bank(ob)
            if sidx < 16:
                for j in range(2):
                    act(Pt[:, j, :], bank(2 * s + j), AF.Exp, [bankB[2 * s + j]], [PB[s][j]], scale=SC)
                for j in range(2):
                    kb = 2 * sidx + j
                    mm(po[0:65, :], T["V_all"][:, kb, h, :], Pt[:, j, :], kb == 0, False, [PB[s][j]], [bankB[ob]])
            else:
                act(Pt[0:16, 0, :], bank(2 * s)[0:16, :], AF.Exp, [bankB[2 * s]], [PB[s][0]], scale=SC)
                mm(po[0:65, :], T["V_all"][0:16, 32, h, :], Pt[0:16, 0, :], False, True, [PB[s][0]], [bankB[ob]])
                us = u % 2
                cp("dve", T["sr%d" % us][64:65, :], po[64:65, :], [bankB[ob]], [srB[us]])

        def mla_norm(u, qb, h):
            us = u % 2
            ob = 4 + us
            pbk = 6 + us
            mm(bank(pbk)[0:64, :], T["ones32"][64:65, 0:64], T["sr%d" % us][64:65, :], True, True, [srB[us], constB], [bankB[pbk]])
            recip(T["rb%d" % us][0:64, :], bank(pbk)[0:64, :], [bankB[pbk]], [rbB[us]])
            tt("dve", T["o_mlaT"][0:64, h, qb * 512:(qb + 1) * 512], bank(ob)[0:64, :], T["rb%d" % us][0:64, :], ALU.mult,
               [bankB[ob], rbB[us]], [])

        mla_qk(0, *steps[0])
        for i, stp in enumerate(steps):
            if i + 1 < len(steps):
                mla_qk(i + 1, *steps[i + 1])
            mla_rest(i, *stp)
            u, qb, h, sidx = stp
            if sidx == 2 and u > 0:
                mla_norm(u - 1, *[(q_, h_) for q_ in range(4) for h_ in range(8)][u - 1])
        mla_norm(31, 3, 7)
        dump("o_mlaT", T["o_mlaT"][0:64], [64, 8, 2048], BF16, [])
        p.barrier()

        p.enabled = upto >= 5
        tag[0] = "c"
        dma_cast(T["wNA"], wview(w_na_out_d), [wNAB])
        dma_cast(T["wG"], wview(w_in_d)[:, :, 2208:4256], [wGB])
        dma_cast(T["wMLA"][0:64], wview(w_mla_out_d, 64), [wMLAB])
        dma_cast(T["wO"], wview(w_out_d), [wOB])
        hgB = [Buf(), Buf()]
        sgnB = [Buf(), Buf()]; sgmB = [Buf(), Buf()]; t1B = [Buf(), Buf()]; t2B = [Buf(), Buf()]

        def p5_xf(g, j):
            lt = g * 4 + j
            xa, xb = xload(x_d[lt * 128:(lt + 1) * 128, :], 128)
            return norm_front(xa, xb, 128)

        def p5_xb(g, j, s2):
            norm_back(s2, 128, "gmix", T["hnTg%d" % (g % 2)], hgB[g % 2], ntok0=j * 128)

        def p5_c(g):
            s = g % 2
            hT, hB_ = T["hnTg%d" % s], hgB[s]
            tk = slice(g * 512, (g + 1) * 512)
            for c8 in range(8):
                r2 = c8 % 2
                bgn, bgm, bpn, bpm = 0 + r2, 2 + r2, 4 + r2, 6 + r2
                for k in range(8):
                    mm(bank(bgn), T["wG"][:, k, c8 * 128:(c8 + 1) * 128], hT[:, k, :], k == 0, k == 7, [wGB, hB_], [bankB[bgn]])
                for k in range(8):
                    mm(bank(bgm), T["wG"][:, k, (8 + c8) * 128:(9 + c8) * 128], hT[:, k, :], k == 0, k == 7, [wGB, hB_], [bankB[bgm]])
                act(T["sgn%d" % r2], bank(bgn), AF.Sigmoid, [bankB[bgn]], [sgnB[r2]])
                act(T["sgm%d" % r2], bank(bgm), AF.Sigmoid, [bankB[bgm]], [sgmB[r2]])
                for kc in range(4):
                    mm(bank(bpn), T["wNA"][:, kc, c8 * 128:(c8 + 1) * 128], T["o_naT"][:, kc, tk], kc == 0, kc == 3, [wNAB], [bankB[bpn]])
                for h in range(8):
                    mm(bank(bpm), T["wMLA"][0:64, h, c8 * 128:(c8 + 1) * 128], T["o_mlaT"][0:64, h, tk], h == 0, h == 7, [wMLAB], [bankB[bpm]])
                tt("dve", T["t1_%d" % r2], bank(bpn), T["sgn%d" % r2], ALU.mult, [bankB[bpn], sgnB[r2]], [t1B[r2]])
                tt("dve", T["t2_%d" % r2], bank(bpm), T["sgm%d" % r2], ALU.mult, [bankB[bpm], sgmB[r2]], [t2B[r2]])
                tt("pool", T["mergedT"][:, c8, tk], T["t1_%d" % r2], T["t2_%d" % r2], ALU.add, [t1B[r2], t2B[r2]], [])
                if g + 1 < 4:
                    if c8 % 2 == 0:
                        pend[0] = p5_xf(g + 1, c8 // 2)
                    else:
                        p5_xb(g + 1, c8 // 2, pend[0])

        pend = [None]
        for j in range(4):
            p5_xb(0, j, p5_xf(0, j))
        for g in range(4):
            p5_c(g)
        dump("mergedT", T["mergedT"], [128, 8, 2048], BF16, [])
        p.barrier()

        p.enabled = upto >= 6
        w1B = [Buf(), Buf()]; w2B = [Buf(), Buf()]

        def load_ffn(G):
            s = G % 2
            dma_cast(T["w1g%d" % s], wview(w_ff1_d)[:, :, G * 512:(G + 1) * 512], [w1B[s]])
            dma_cast(T["w2g%d" % s], wview(w_ff2_d)[:, G * 4:(G + 1) * 4, :], [w2B[s]])

        load_ffn(0)
        hB = [Buf() for _ in range(16)]
        fnTB = Buf()
        def p6_a(lt):
            xa, xb = xload(x_d[lt * 128:(lt + 1) * 128, :], 128)
            for half in range(2):
                b = next_bank(0, 6)
                for c in range(8):
                    mm(bank(b), T["mergedT"][:, c, lt * 128:(lt + 1) * 128], T["wO"][:, c, half * 512:(half + 1) * 512],
                       c == 0, c == 7, [wOB], [bankB[b]])
                tt("dve", T["h_acc"][:, lt, half * 512:(half + 1) * 512], bank(b), xa[:, half * 512:(half + 1) * 512], ALU.add,
                   [bankB[b], xb], [hB[lt]])

        p6_a(0)
        for lt in range(16):
            if lt + 1 < 16:
                p6_a(lt + 1)
            norm_T(T["h_acc"][:, lt, :], hB[lt], 128, "gffn", T["fnT"], fnTB, ntok0=lt * 128)
        dump("h1", T["h_acc"], [128, 16, 1024], F32, hB)
        p.barrier()

        p.enabled = upto >= 7
        load_ffn(1)
        gfinB = Buf()
        dma_sp(T["gfin"], gfin_d, [], [gfinB])
        rB = [Buf(), Buf()]; uTB = [Buf(), Buf()]
        fsteps = [(G, tg) for G in range(8) for tg in range(4)]

        def ff1(i, G, tg):
            s, ws = i % 2, G % 2
            for f in range(4):
                b = next_bank(0, 4)
                for k in range(8):
                    mm(bank(b), T["w1g%d" % ws][:, k, f * 128:(f + 1) * 128], T["fnT"][:, k, tg * 512:(tg + 1) * 512],
                       k == 0, k == 7, [w1B[ws]], [bankB[b]])
                rs = rot["x"] % 2
                rot["x"] += 1
                act(T["r%d" % rs], bank(b), AF.Relu, [bankB[b]], [rB[rs]])
                tt("pool", T["uT%d" % s][:, f, :], T["r%d" % rs], T["r%d" % rs], ALU.mult, [rB[rs]], [uTB[s]])

        def ff2(i, G, tg):
            s, ws = i % 2, G % 2
            for j in range(4):
                lt = tg * 4 + j
                for half in range(2):
                    b = 4 + rot["pt"] % 4
                    rot["pt"] += 1
                    for f in range(4):
                        mm(bank(b), T["uT%d" % s][:, f, j * 128:(j + 1) * 128], T["w2g%d" % ws][:, f, half * 512:(half + 1) * 512],
                           f == 0, f == 3, [uTB[s], w2B[ws]], [bankB[b]])
                    hv = T["h_acc"][:, lt, half * 512:(half + 1) * 512]
                    tt("dve", hv, bank(b), hv, ALU.add, [bankB[b], hB[lt]], [hB[lt]])

        ff1(0, *fsteps[0])
        for i, (G, tg) in enumerate(fsteps):
            if i + 1 < len(fsteps):
                ff1(i + 1, *fsteps[i + 1])
            ff2(i, G, tg)
            if tg == 3 and G + 2 < 8:
                load_ffn(G + 2)
        p.barrier()

        p.enabled = upto >= 8
        tag[0] = "d"
        otB = [Buf(), Buf()]
        outB = []
        fst = T["fst"]
        fsB = [Buf(), Buf(), Buf()]
        for lt in range(16):
            s = lt % 2
            act(T["ot%d" % s], T["h_acc"][:, lt, :], AF.Square, [hB[lt]], [fsB[0], otB[s]], accum=fst[:, lt:lt + 1])
        ts("dve", fst[:, 16:32], fst[:, 0:16], 1.0 / 1024, EPS, ALU.mult, ALU.add, [fsB[0]], [fsB[1]])
        tt("pool", fst[:, 32:48], fst[:, 16:32], T["neghalf"][:, 0:1].broadcast_to([128, 16]), ALU.pow, [fsB[1], constB], [fsB[2]])
        for lt in range(16):
            s = lt % 2
            ot = T["ot%d" % s]
            p.op("dve", (lambda ot, hv, sc: (lambda e: e.scalar_tensor_tensor(out=ot, in0=hv, scalar=sc, in1=T["gfin"], op0=ALU.mult, op1=ALU.mult)))(ot, T["h_acc"][:, lt, :], fst[:, 32 + lt:33 + lt]),
                 [hB[lt], fsB[2], gfinB], [otB[s]])
            ob_ = Buf()
            outB.append(ob_)
            dma_sp(out_d[lt * 128:(lt + 1) * 128, :], ot, [otB[s]], [ob_], semkey=otB[s])
        p.enabled = True
        p.op("sp", None, outB + list(dbg.values()))
        p.emit(es)
    return nc


def _perm(half):
    t = np.arange(4096)
    if half == 0:
        return t
    r, c = t // 64, t % 64
    return (63 - r) * 64 + c


def _na_tables(rpb, half):
    a = np.arange(2)[:, None, None, None]
    kc = np.arange(64)[None, :, None, None]
    i = np.arange(2)[None, None, :, None]
    c = np.arange(64)[None, None, None, :]

    def tab(m, pairs):
        out = np.full((8, len(pairs), 2, 64, 2, 64), NEG, np.float32)
        for pi, lp in enumerate(pairs):
            rk_l = 2 * lp + a
            rq_l = 2 * m + i
            rk = rk_l if half == 0 else 63 - rk_l
            rq = rq_l if half == 0 else 63 - rq_l
            rs = np.clip(rq - 4, 0, 56)
            rvalid = (rk >= rs) & (rk <= rs + 7)
            cs = np.clip(c - 8, 0, 48)
            cvalid = (kc >= cs) & (kc < cs + 16)
            valid = np.broadcast_to(rvalid & cvalid, (2, 64, 2, 64))
            dr = np.clip(rk - rq + 7, 0, 14)
            dc = np.clip(kc - c + 15, 0, 30)
            dr_b = np.broadcast_to(dr, (2, 64, 2, 64))
            dc_b = np.broadcast_to(dc, (2, 64, 2, 64))
            vals = rpb[:, dr_b, dc_b]
            out[:, pi] = np.where(valid[None], vals, np.float32(NEG))
        return out.reshape(8, len(pairs), 128, 128).transpose(2, 0, 1, 3)

    tabI = tab(4, [2, 3, 4, 5, 6])
    tabB = np.stack([tab(0, [0, 1, 2, 3]), tab(1, [0, 1, 2, 3])], axis=1)
    return np.ascontiguousarray(tabI).reshape(128, -1), np.ascontiguousarray(tabB).reshape(128, -1)


_CACHE = {}


def kernel(x, meta, norm_mix, w_in, na_rpb, mla_q_norm, w_uq, mla_kv_norm, w_ukv,
           w_na_out, w_mla_out, w_out, norm_ffn, w_ff1, w_ff2, norm_final, _debug=False, _cores=None, _upto=99):
    f32 = np.float32
    x = np.asarray(x, f32)
    col = lambda v, n: np.ascontiguousarray(np.asarray(v, f32).reshape(n, 128).T)
    inv_freq = (1.0 / (f32(10000.0) ** (np.arange(0, 32, 2, dtype=f32) / f32(32)))).astype(f32)
    shared = {
        "meta": np.ascontiguousarray(meta, f32),
        "gmix": col(norm_mix[0], 8), "gffn": col(norm_ffn[0], 8),
        "gq": col(mla_q_norm[0], 3), "gkv": col(mla_kv_norm[0], 2),
        "gfin": np.ascontiguousarray(np.broadcast_to(np.asarray(norm_final, f32)[None, :], (128, 1024))),
        "ident": np.eye(128, dtype=f32),
        "w_in": np.ascontiguousarray(w_in[0], f32), "w_uq": np.ascontiguousarray(w_uq[0], f32),
        "w_ukv": np.ascontiguousarray(w_ukv[0], f32), "w_na_out": np.ascontiguousarray(w_na_out[0], f32),
        "w_mla_out": np.ascontiguousarray(w_mla_out[0], f32), "w_out": np.ascontiguousarray(w_out[0], f32),
        "w_ff1": np.ascontiguousarray(w_ff1[0], f32), "w_ff2": np.ascontiguousarray(w_ff2[0], f32),
    }
    per_half = {}
    for half in range(2):
        perm = _perm(half)
        pos = np.concatenate([perm + 16, np.arange(16), np.zeros(112, np.int64)]).astype(f32)
        ang = (pos[:, None] * inv_freq[None, :]).astype(f32)
        cos = np.cos(ang).astype(f32).reshape(33, 128, 16).transpose(1, 0, 2).reshape(128, -1)
        sin = np.sin(ang).astype(f32).reshape(33, 128, 16).transpose(1, 0, 2).reshape(128, -1)
        tabI, tabB = _na_tables(np.asarray(na_rpb[0], f32), half)
        per_half[half] = dict(perm=perm, cos=np.ascontiguousarray(cos), sin=np.ascontiguousarray(sin), tabI=tabI, tabB=tabB)
    cores = list(range(8)) if _cores is None else _cores
    in_maps = []
    for cid in cores:
        b, half = cid // 2, cid % 2
        ph = per_half[half]
        m = dict(shared)
        m["x"] = np.ascontiguousarray(x[b][ph["perm"]])
        m["cos"], m["sin"], m["tabI"], m["tabB"] = ph["cos"], ph["sin"], ph["tabI"], ph["tabB"]
        in_maps.append(m)
    key = (bool(_debug), _upto)
    if key not in _CACHE:
        _CACHE[key] = build(debug=_debug, upto=_upto)
    nc = _CACHE[key]
    res = run_bass_kernel_spmd(nc, in_maps, core_ids=list(range(len(cores))))
    out = np.zeros((4, 4096, 1024), f32)
    for k, cid in enumerate(cores):
        b, half = cid // 2, cid % 2
        out[b, per_half[half]["perm"][:2048]] = res.results[k]["out"]
    if _debug:
        return out, res.results
    return out
```
